# Optimizing a Trainium2 kernel written in Bass

```python
import math
import jax, jax.numpy as jnp
from jax import lax
import numpy as np

D_MODEL = 1024
BATCH = 2
SEQ = 8192
DEPTH = 1

MLA_HEADS = 8
MLA_Q_RANK = 384
MLA_KV_RANK = 256
MLA_NOPE = 64
MLA_ROPE = 32
MLA_V = 64
MLA_SCALE = 1.0 / math.sqrt(MLA_NOPE + MLA_ROPE)
ROPE_THETA = 10000.0
FOX_HEADS = 8
FOX_DIM = 64
FOX_SCALE = 1.0 / math.sqrt(FOX_DIM)
D_FF = 2816
CONV_WIDTH = 3
BLOCK_Q = 128
EPS = 1e-6
N_ADA = 6

IN_SPLITS = (
    MLA_Q_RANK,
    MLA_KV_RANK,
    MLA_ROPE,
    FOX_HEADS * FOX_DIM,
    FOX_HEADS * FOX_DIM,
    FOX_HEADS * FOX_DIM,
    FOX_HEADS,
    D_MODEL,
    D_MODEL,
)
D_IN = sum(IN_SPLITS)
IN_OFFSETS = tuple(int(v) for v in np.cumsum(IN_SPLITS)[:-1])

kernel_name = "hybrid_mla_fox_convffn_adaln"


def rmsnorm(x, g):
    xf = x.astype(jnp.float32)
    y = xf * lax.rsqrt(jnp.mean(xf * xf, axis=-1, keepdims=True) + EPS)
    return (y * g.astype(jnp.float32)).astype(x.dtype)


def rope(x, positions):
    r = x.shape[-1]
    inv_freq = ROPE_THETA ** (-jnp.arange(0, r, 2, dtype=jnp.float32) / r)
    ang = positions.astype(jnp.float32)[..., None] * inv_freq
    cos = jnp.cos(ang)[:, :, None, :]
    sin = jnp.sin(ang)[:, :, None, :]
    xf = x.astype(jnp.float32)
    x1, x2 = xf[..., : r // 2], xf[..., r // 2:]
    out = jnp.concatenate([x1 * cos - x2 * sin, x2 * cos + x1 * sin], axis=-1)
    return out.astype(x.dtype)


def causal_block_attention(q, k, v, scale, log_cum=None):
    b, s, h, dk = q.shape
    nb = s // BLOCK_Q
    idx = jnp.arange(nb)
    qb = q.reshape(b, nb, BLOCK_Q, h, dk).transpose(1, 0, 2, 3, 4)
    key_pos = jnp.arange(s)
    if log_cum is not None:
        f_keys = log_cum.transpose(0, 2, 1)
        fb = log_cum.reshape(b, nb, BLOCK_Q, h).transpose(1, 0, 2, 3)
        xs = (idx, qb, fb)
    else:
        xs = (idx, qb)

    def one_block(args):
        i, qi = args[0], args[1]
        logits = jnp.einsum('bqhd,bkhd->bhqk', qi, k,
                            preferred_element_type=jnp.float32) * scale
        if log_cum is not None:
            fi = args[2].transpose(0, 2, 1)
            logits = logits + (fi[..., :, None] - f_keys[:, :, None, :])
        q_pos = i * BLOCK_Q + jnp.arange(BLOCK_Q)
        mask = key_pos[None, :] <= q_pos[:, None]
        logits = jnp.where(mask[None, None], logits, -jnp.inf)
        p = jax.nn.softmax(logits, axis=-1)
        return jnp.einsum('bhqk,bkhd->bqhd', p.astype(v.dtype), v)

    out = lax.map(one_block, xs)
    return out.transpose(1, 0, 2, 3, 4).reshape(b, s, h, v.shape[-1])


def hybrid_mixer(h, positions, w_in, q_norm_g, w_uq, kv_norm_g, w_ukv, b_forget,
                 w_o_mla, w_o_fox, w_out):
    b, s, _ = h.shape
    proj = h @ w_in
    cq, ckv, k_rope, fq, fk, fv, f_logit, g_mla, g_fox = jnp.split(proj, IN_OFFSETS, axis=-1)

    q = (rmsnorm(cq, q_norm_g) @ w_uq).reshape(b, s, MLA_HEADS, MLA_NOPE + MLA_ROPE)
    q_nope, q_rope = q[..., :MLA_NOPE], q[..., MLA_NOPE:]
    kv = (rmsnorm(ckv, kv_norm_g) @ w_ukv).reshape(b, s, MLA_HEADS, MLA_NOPE + MLA_V)
    k_nope, v_mla = kv[..., :MLA_NOPE], kv[..., MLA_NOPE:]
    q_rope = rope(q_rope, positions)
    k_rope = rope(k_rope[:, :, None, :], positions)
    q_mla = jnp.concatenate([q_nope, q_rope], axis=-1)
    k_mla = jnp.concatenate(
        [k_nope, jnp.broadcast_to(k_rope, (b, s, MLA_HEADS, MLA_ROPE))], axis=-1)
    o_mla = causal_block_attention(q_mla, k_mla, v_mla, MLA_SCALE).reshape(b, s, MLA_HEADS * MLA_V)

    log_f = jax.nn.log_sigmoid(f_logit.astype(jnp.float32) + b_forget.astype(jnp.float32))
    f_cum = jnp.cumsum(log_f, axis=1)
    o_fox = causal_block_attention(
        fq.reshape(b, s, FOX_HEADS, FOX_DIM), fk.reshape(b, s, FOX_HEADS, FOX_DIM),
        fv.reshape(b, s, FOX_HEADS, FOX_DIM), FOX_SCALE, f_cum).reshape(b, s, FOX_HEADS * FOX_DIM)

    y = jax.nn.sigmoid(g_mla) * (o_mla @ w_o_mla) + jax.nn.sigmoid(g_fox) * (o_fox @ w_o_fox)
    return y @ w_out


def conv_ffn(h, w_up, conv_w, conv_b, w_down):
    s = h.shape[1]
    u = h @ w_up
    up = jnp.pad(u, ((0, 0), (CONV_WIDTH - 1, 0), (0, 0)))
    u = conv_b + sum(conv_w[j] * up[:, j:j + s] for j in range(CONV_WIDTH))
    gate, val = u[..., :D_FF], u[..., D_FF:]
    return (jax.nn.silu(gate) * val) @ w_down


def modulate(hn, shift, scale):
    return hn * (1.0 + scale[:, None, :]) + shift[:, None, :]


def setup_inputs(seed: int = 0) -> dict:
    key = jax.random.key(seed)
    ks = jax.random.split(key, 24)
    nrm = lambda k, shape, fan: jax.random.normal(k, shape, jnp.float32) * (fan ** -0.5)
    L = DEPTH
    x = jax.random.normal(ks[0], (BATCH, SEQ, D_MODEL), jnp.float32)
    c = jax.random.normal(ks[1], (BATCH, D_MODEL), jnp.float32)
    offsets = jax.random.randint(ks[2], (BATCH, 1), 0, 4096, dtype=jnp.int32)
    positions = offsets + jnp.arange(SEQ, dtype=jnp.int32)[None, :]
    return {
        "x": x,
        "c": c,
        "positions": positions,
        "w_ada": 0.5 * nrm(ks[3], (L, D_MODEL, N_ADA * D_MODEL), D_MODEL),
        "b_ada": 0.01 * jax.random.normal(ks[4], (L, N_ADA * D_MODEL), jnp.float32),
        "norm_mix_g": 1.0 + 0.1 * jax.random.normal(ks[5], (L, D_MODEL), jnp.float32),
        "w_in": nrm(ks[6], (L, D_MODEL, D_IN), D_MODEL),
        "q_norm_g": 1.0 + 0.1 * jax.random.normal(ks[7], (L, MLA_Q_RANK), jnp.float32),
        "w_uq": nrm(ks[8], (L, MLA_Q_RANK, MLA_HEADS * (MLA_NOPE + MLA_ROPE)), MLA_Q_RANK),
        "kv_norm_g": 1.0 + 0.1 * jax.random.normal(ks[9], (L, MLA_KV_RANK), jnp.float32),
        "w_ukv": nrm(ks[10], (L, MLA_KV_RANK, MLA_HEADS * (MLA_NOPE + MLA_V)), MLA_KV_RANK),
        "b_forget": jax.random.uniform(ks[11], (L, FOX_HEADS), jnp.float32, 1.0, 6.0),
        "w_o_mla": nrm(ks[12], (L, MLA_HEADS * MLA_V, D_MODEL), MLA_HEADS * MLA_V),
        "w_o_fox": nrm(ks[13], (L, FOX_HEADS * FOX_DIM, D_MODEL), FOX_HEADS * FOX_DIM),
        "w_out": nrm(ks[14], (L, D_MODEL, D_MODEL), D_MODEL),
        "norm_ffn_g": 1.0 + 0.1 * jax.random.normal(ks[15], (L, D_MODEL), jnp.float32),
        "w_up": nrm(ks[16], (L, D_MODEL, 2 * D_FF), D_MODEL),
        "conv_w": nrm(ks[17], (L, CONV_WIDTH, 2 * D_FF), CONV_WIDTH),
        "conv_b": 0.01 * jax.random.normal(ks[18], (L, 2 * D_FF), jnp.float32),
        "w_down": nrm(ks[19], (L, D_FF, D_MODEL), D_FF),
        "norm_final_g": 1.0 + 0.1 * jax.random.normal(ks[20], (D_MODEL,), jnp.float32),
    }


def reference(x, c, positions, w_ada, b_ada, norm_mix_g, w_in, q_norm_g, w_uq, kv_norm_g,
              w_ukv, b_forget, w_o_mla, w_o_fox, w_out, norm_ffn_g, w_up, conv_w, conv_b,
              w_down, norm_final_g):
    c_act = jax.nn.silu(c)
    for l in range(DEPTH):
        ada = c_act @ w_ada[l] + b_ada[l]
        sh_m, sc_m, g_m, sh_f, sc_f, g_f = jnp.split(ada, N_ADA, axis=-1)
        h = modulate(rmsnorm(x, norm_mix_g[l]), sh_m, sc_m)
        mix = hybrid_mixer(h, positions, w_in[l], q_norm_g[l], w_uq[l], kv_norm_g[l], w_ukv[l],
                           b_forget[l], w_o_mla[l], w_o_fox[l], w_out[l])
        x = x + g_m[:, None, :] * mix
        h = modulate(rmsnorm(x, norm_ffn_g[l]), sh_f, sc_f)
        x = x + g_f[:, None, :] * conv_ffn(h, w_up[l], conv_w[l], conv_b[l], w_down[l])
    return rmsnorm(x, norm_final_g)
```

```python
import os
import numpy as np
from contextlib import ExitStack
import concourse.bass as bass
import concourse.mybir as mybir
from concourse.bass_utils import run_bass_kernel_spmd

F32 = mybir.dt.float32
BF16 = mybir.dt.bfloat16
I32 = mybir.dt.int32
AF = mybir.ActivationFunctionType
ALU = mybir.AluOpType

D = 1024
S = 8192
NBLK = 16
SBW = 130
TOWN = NBLK * SBW
QT = 2 * SBW
NQT = 8
DFF = 2816
NFF = 22
EPS = 1e-6
NEG = -30000.0
MLA_SCALE = float(1.0 / np.sqrt(96.0))
SB_BASE = 16512
SB_LIMIT = 229248


class Op:
    __slots__ = ("eng", "fn", "deps", "needs_inc", "semval", "is_dma", "dsem", "dval")

    def __init__(self, eng, fn, is_dma=False):
        self.eng = eng
        self.fn = fn
        self.deps = []
        self.needs_inc = False
        self.semval = 0
        self.is_dma = is_dma
        self.dsem = None
        self.dval = 0


class Prog:
    ENGS = ("pe", "act", "dve", "pool", "sp")
    NDS = 16

    def __init__(self, nc):
        self.nc = nc
        self.ops = {e: [] for e in self.ENGS}
        self.last_write = {}
        self.readers = {}
        self.ndma = {e: 0 for e in self.ENGS}
        self.dma_ops = {e: [] for e in self.ENGS}

    def add(self, eng, fn, reads=(), writes=(), dma=False):
        op = Op(eng, fn, dma)
        deps = []
        for r in reads:
            lw = self.last_write.get(r)
            if lw is not None:
                deps.append(lw)
        for w in writes:
            lw = self.last_write.get(w)
            if lw is not None:
                deps.append(lw)
            deps.extend(self.readers.get(w, ()))
        seen = set()
        for d in deps:
            if id(d) not in seen:
                seen.add(id(d))
                op.deps.append(d)
        for r in reads:
            self.readers.setdefault(r, []).append(op)
        for w in writes:
            self.last_write[w] = op
            self.readers[w] = []
        if dma:
            i = self.ndma[eng]
            self.ndma[eng] += 1
            op.dsem = i % self.NDS
            op.dval = 16 * (i // self.NDS + 1)
            if i >= self.NDS:
                op.deps.append(self.dma_ops[eng][i - self.NDS])
            self.dma_ops[eng].append(op)
        self.ops[eng].append(op)
        return op

    def barrier(self):
        lasts = []
        for e in self.ENGS:
            real = [o for o in self.ops[e][-64:] if o.fn is not None and not o.is_dma]
            if real:
                lasts.append(real[-1])
            lasts.extend(self.dma_ops[e][-self.NDS:])
        for e in self.ENGS:
            op = Op(e, None)
            op.deps = list(lasts)
            self.ops[e].append(op)
        self.last_write = {}
        self.readers = {}

    def emit(self):
        nc = self.nc
        for e in self.ENGS:
            for op in self.ops[e]:
                for d in op.deps:
                    if d.is_dma:
                        continue
                    if d.eng == e and e == "pe":
                        continue
                    d.needs_inc = True
        for e in self.ENGS:
            c = 0
            for op in self.ops[e]:
                if op.needs_inc and not op.is_dma:
                    c += 1
                    op.semval = c
        with ExitStack() as st:
            sems = {e: st.enter_context(nc.semaphore("s_" + e)) for e in self.ENGS}
            dsems = {e: [st.enter_context(nc.semaphore("d_%s%d" % (e, i))) for i in range(self.NDS)]
                     for e in ("sp", "pool")}
            block = st.enter_context(nc.Block())
            engobj = {"pe": block.tensor, "act": block.scalar, "dve": block.vector,
                      "pool": block.gpsimd, "sp": block.sync}

            def body(e):
                def run(eng):
                    waited = {}
                    for op in self.ops[e]:
                        for d in op.deps:
                            if d.is_dma:
                                key = ("d", d.eng, d.dsem)
                                sem = dsems[d.eng][d.dsem]
                                val = d.dval
                            else:
                                if d.eng == e and e == "pe":
                                    continue
                                key = ("e", d.eng)
                                sem = sems[d.eng]
                                val = d.semval
                            if waited.get(key, 0) >= val:
                                continue
                            waited[key] = val
                            eng.wait_ge(sem, val)
                        if op.fn is None:
                            continue
                        inst = op.fn(eng)
                        if op.is_dma:
                            inst.then_inc(dsems[e][op.dsem], 16)
                        elif op.needs_inc:
                            inst.then_inc(sems[e], 1)
                return run

            for e in self.ENGS:
                engobj[e](body(e))


class Arena:
    def __init__(self, nc):
        self.nc = nc
        self.off = SB_BASE
        self.top = SB_LIMIT
        self.n = 0

    def alloc(self, name, shape, dt):
        sz = int(np.prod(shape[1:])) * mybir.dt.size(dt)
        sz = (sz + 63) // 64 * 64
        assert self.off + sz <= self.top, ("SBUF overflow", name, self.off, sz, self.top)
        self.n += 1
        t = self.nc.alloc_sbuf_tensor_at("%s_%d" % (name, self.n), list(shape), dt, offset=self.off)
        self.off += sz
        return t

    def alloc_top(self, name, shape, dt):
        sz = int(np.prod(shape[1:])) * mybir.dt.size(dt)
        sz = (sz + 63) // 64 * 64
        self.top -= sz
        assert self.top >= self.off, ("SBUF overflow(top)", name, self.off, self.top)
        self.n += 1
        return self.nc.alloc_sbuf_tensor_at("%s_%d" % (name, self.n), list(shape), dt, offset=self.top)

    def mark(self):
        return self.off

    def reset(self, m):
        self.off = m


def wview(ap, p=128):
    return ap.rearrange("(k p) c -> p k c", p=p)


class Builder:
    def __init__(self, stop_after=None, dbg=False):
        self.stop_after = stop_after
        self.dbg = dbg
        self.dump_scr = []
        self.nc = nc = bass.Bass("TRN2", target_bir_lowering=False)
        self.P = Prog(nc)
        self.A = Arena(nc)
        self.din = {}
        self.scr = {}
        self.rot = 0
        self.alt = 0

    def inp(self, name, shape, dt=F32):
        self.din[name] = self.nc.dram_tensor(name, list(shape), dt, kind="ExternalInput").ap()
        return self.din[name]

    def scratch(self, name, shape, dt):
        ap = self.nc.dram_tensor(name, list(shape), dt, kind="Internal").ap()
        self.scr[name] = ap
        return ap

    def dma(self, q, out, in_, reads=(), writes=(), **kw):
        return self.P.add(q, lambda e: e.dma_start(out=out, in_=in_, **kw), reads=reads, writes=writes, dma=True)

    def evac_engine(self):
        self.alt ^= 1
        return "act" if self.alt else "dve"

    def evac(self, eng, out, in_, scale=None, bias=None, reads=(), writes=()):
        if eng == "act":
            def fn(e):
                kw = {}
                if scale is not None:
                    kw["scale"] = scale
                if bias is not None:
                    kw["bias"] = bias
                func = AF.Identity if (bias is not None and not isinstance(bias, float)) or (scale is not None and not isinstance(scale, float)) else AF.Copy
                return e.activation(out=out, in_=in_, func=func, **kw)
        else:
            def fn(e):
                if scale is None and bias is None:
                    return e.tensor_copy(out=out, in_=in_)
                if bias is None:
                    return e.tensor_scalar(out=out, in0=in_, scalar1=scale, scalar2=None, op0=ALU.mult)
                s = 1.0 if scale is None else scale
                return e.tensor_scalar(out=out, in0=in_, scalar1=s, scalar2=bias, op0=ALU.mult, op1=ALU.add)
        return self.P.add(eng, fn, reads=reads, writes=writes)

    def bank_rot(self, lo=4, n=4):
        b = lo + self.rot % n
        self.rot += 1
        return b

    def mm_group(self, out, pairs, reads=(), writes=()):
        def fn(e):
            inst = None
            n = len(pairs)
            for i, (l, r) in enumerate(pairs):
                inst = e.matmul(out, lhsT=l, rhs=r, start=(i == 0), stop=(i == n - 1))
            return inst
        return self.P.add("pe", fn, reads=reads, writes=writes)

    def build(self):
        nc, P, A = self.nc, self.P, self.A
        inp = self.inp
        x_kv = inp("x_kv", [S, D])
        x_own = inp("x_own", [TOWN, D])
        pos_kv = inp("pos_kv", [1, S], I32)
        pos_own = inp("pos_own", [1, TOWN], I32)
        c_col = inp("c_col", [128, 8])
        w_ada = inp("w_ada", [D, 6 * D])
        b_ada_c = inp("b_ada_c", [128, 48])
        g_mix_c = inp("g_mix_c", [128, 8])
        g_ffn_c = inp("g_ffn_c", [128, 8])
        g_fin_r = inp("g_fin_r", [1, D])
        g_q_c = inp("g_q_c", [128, 3])
        g_kv_c = inp("g_kv_c", [128, 2])
        b_fg_c = inp("b_fg_c", [128, 1])
        conv_w_c = inp("conv_w_c", [128, 44, 3])
        conv_b_c = inp("conv_b_c", [128, 44])
        w_ckv = inp("w_ckv", [D, 256])
        w_kr = inp("w_kr", [D, 96])
        w_krs = inp("w_krs", [D, 96])
        w_fk = inp("w_fk", [D, 512])
        w_fv = inp("w_fv", [D, 512])
        w_fl = inp("w_fl", [D, 8])
        w_cq = inp("w_cq", [D, 384])
        w_fq = inp("w_fq", [D, 512])
        w_gm = inp("w_gm", [D, D])
        w_gf = inp("w_gf", [D, D])
        w_uq = inp("w_uq", [384, 768])
        w_uqs = inp("w_uqs", [384, 768])
        w_ukv = inp("w_ukv", [256, 1024])
        w_om = inp("w_om", [512, D])
        w_of = inp("w_of", [512, D])
        w_out = inp("w_out", [D, D])
        w_up = inp("w_up", [D, 2 * DFF])
        w_down = inp("w_down", [DFF, D])
        masks = inp("masks", [128, 4, SBW])
        hm1 = inp("hm1", [128, 2])
        flags = inp("flags", [128, 4])
        hvalid = inp("hvalid", [128, 1])
        invf_c = inp("invf_c", [96, 1])
        sgn_c = inp("sgn_c", [96, 1])
        ident_d = inp("ident", [128, 128])
        cmat_d = inp("cmat", [128, 128])
        out_d = nc.dram_tensor("out", [NBLK * 128, D], F32, kind="ExternalOutput").ap()
        self.out_d = out_d

        KMd = self.scratch("KMd", [8, 64, S], BF16)
        KRd = self.scratch("KRd", [32, S], BF16)
        VMd = self.scratch("VMd", [8, 64, S], BF16)
        KFd = self.scratch("KFd", [8, 67, S], BF16)
        VFd = self.scratch("VFd", [8, 64, S], BF16)
        FQd = self.scratch("FQd", [3, 128, SBW], BF16)
        HTd = self.scratch("HTd", [128, 8, TOWN], BF16)
        QFd = self.scratch("QFd", [8, 64, TOWN], BF16)
        X1d = self.scratch("X1d", [128, 8, TOWN], F32)

        psS = [nc.alloc_psum_tensor("psS%d" % i, [128, 2, 512], F32) for i in range(3)]
        psB = [nc.alloc_psum_tensor("psB%d" % i, [128, 512], F32) for i in range(2)]
        B = [psS[i][:, j, :] for i in range(3) for j in range(2)] + [t[:] for t in psB]
        self.B = B
        self.psS = psS
        self.psB_raw = psB

        ident_f = A.alloc("ident_f", [128, 128], F32)
        ident_b = A.alloc("ident_b", [128, 128], BF16)
        ones_f = A.alloc("ones_f", [128, 512], F32)
        mhalf = A.alloc("mhalf", [128, 8], F32)
        ada = A.alloc("ada", [128, 48], F32)
        gs1 = A.alloc("gs1", [128, 8], F32)
        gs2 = A.alloc("gs2", [128, 8], F32)
        gq = A.alloc("gq", [128, 3], F32)
        gkv = A.alloc("gkv", [128, 2], F32)
        invf = A.alloc("invf", [96, 1], F32)
        sgn = A.alloc("sgn", [96, 1], F32)
        flg = A.alloc("flg", [128, 4], F32)
        hval = A.alloc("hval", [128, 1], F32)
        bada = A.alloc("bada", [128, 48], F32)
        gffn = A.alloc("gffn", [128, 8], F32)
        cact = A.alloc("cact", [128, 8], F32)
        maskb = A.alloc("maskb", [128, 4, SBW], BF16)
        hm1b = A.alloc("hm1b", [128, 2], BF16)
        self.consts = dict(ident_f=ident_f, ident_b=ident_b, ones_f=ones_f)

        self.dma("sp", ident_f[:], ident_d, writes=["ident_f"])
        self.dma("pool", ident_b[:], ident_d, writes=["ident_b"])
        self.dma("pool", maskb[:], masks, writes=["maskb"])
        self.dma("pool", hm1b[:], hm1, writes=["hm1b"])
        self.dma("sp", gq[:], g_q_c, writes=["gq"])
        self.dma("sp", gkv[:], g_kv_c, writes=["gkv"])
        self.dma("sp", invf[:], invf_c, writes=["invf"])
        self.dma("sp", sgn[:], sgn_c, writes=["sgn"])
        self.dma("sp", flg[:], flags, writes=["flg"])
        self.dma("sp", hval[:], hvalid, writes=["hval"])
        P.add("pool", lambda e: e.memset(ones_f[:], 1.0), writes=["ones_f"])
        P.add("pool", lambda e: e.memset(mhalf[:], -0.5), writes=["mhalf"])
        self.mhalf = mhalf

        persist_mark = A.mark()
        if self.stop_after == "C":
            return self.finish_dbg({"maskb": (maskb, [128, 4, SBW], BF16)})

        c_sb = A.alloc("c_sb", [128, 8], F32)
        gmix = A.alloc("gmix", [128, 8], F32)
        tmp8 = A.alloc("tmp8", [128, 8], F32)
        wab = [A.alloc("wab%d" % i, [128, 8, D], F32) for i in range(2)]
        self.dma("sp", c_sb[:], c_col, writes=["c_sb"])
        self.dma("sp", bada[:], b_ada_c, writes=["bada"])
        self.dma("sp", gmix[:], g_mix_c, writes=["gmix"])
        self.dma("sp", gffn[:], g_ffn_c, writes=["gffn"])
        P.add("act", lambda e: e.activation(out=cact[:], in_=c_sb[:], func=AF.Silu), reads=["c_sb"], writes=["cact"])
        for v in range(2):
            wb = wab[v % 2]
            self.dma("sp" if v % 2 == 0 else "pool", wb[:], wview(w_ada[:, v * D:(v + 1) * D]), writes=["wab%d" % (v % 2)])

            def fn(e, v=v, wb=wb):
                inst = None
                for j in range(8):
                    for k in range(8):
                        inst = e.matmul(B[4][:, v * 8 + j:v * 8 + j + 1], lhsT=wb[:, k, j * 128:(j + 1) * 128],
                                        rhs=cact[:, k:k + 1], start=(k == 0), stop=(k == 7))
                return inst
            P.add("pe", fn, reads=["wab%d" % (v % 2), "cact"], writes=["B4"])
        P.add("dve", lambda e: e.tensor_tensor(out=ada[:, 0:16], in0=B[4][:, 0:16], in1=bada[:, 0:16], op=ALU.add),
              reads=["B4", "bada"], writes=["ada"])
        P.add("dve", lambda e: e.tensor_scalar_add(out=tmp8[:], in0=ada[:, 8:16], scalar1=1.0), reads=["ada"], writes=["tmp8"])
        P.add("dve", lambda e: e.tensor_tensor(out=gs1[:], in0=tmp8[:], in1=gmix[:], op=ALU.mult), reads=["tmp8", "gmix"], writes=["gs1"])
        sh1 = ada[:, 0:8]
        sh2 = ada[:, 24:32]
        gm = ada[:, 16:24]
        gf = ada[:, 40:48]
        P.barrier()
        A.reset(persist_mark)
        if self.stop_after == "A":
            return self.finish_dbg({"ada": (ada, [128, 48], F32)})

        self.qw = dict(wcq=A.alloc_top("wcq", [128, 8, 384], BF16), wfq=A.alloc_top("wfq", [128, 8, 512], BF16),
                       wuq=A.alloc_top("wuq", [128, 3, 768], BF16), wuqs=A.alloc_top("wuqs", [128, 3, 768], BF16))
        self.ada_ctx = dict(w_ada=w_ada, ada=ada, bada=bada, cact=cact, gffn=gffn, gs2=gs2)
        self.phase_kv(x_kv, pos_kv, w_ckv, w_kr, w_krs, w_fk, w_fv, w_fl, w_ukv, b_fg_c, cmat_d,
                      gs1, sh1, gkv, invf, sgn, flg, KMd, KRd, VMd, KFd, VFd, FQd)
        for t, d, nm in ((self.qw["wcq"], w_cq, "wcq"), (self.qw["wfq"], w_fq, "wfq"), (self.qw["wuq"], w_uq, "wuq"), (self.qw["wuqs"], w_uqs, "wuqs")):
            pass
        P.barrier()
        A.reset(persist_mark)
        if self.stop_after == "KV":
            self.dump_scr = ["KMd", "KRd", "VMd", "KFd", "VFd", "FQd"]
            return self.finish_dbg({})

        Qreg = [A.alloc("Q%d" % h, [128, TOWN], BF16) for h in range(8)]
        q_mark = A.mark()
        self.phase_q(x_own, pos_own, w_cq, w_fq, w_uq, w_uqs, gs1, sh1, gq, invf, sgn, Qreg, HTd, QFd)
        P.barrier()
        A.reset(q_mark)
        A.top = SB_LIMIT
        if self.stop_after == "Q":
            self.dump_scr = ["QFd", "HTd"]
            return self.finish_dbg({"Q%d" % h: (Qreg[h], [96, TOWN], BF16) for h in (0, 5)})

        oT = [[A.alloc("oT%d_%d" % (br, hp), [128, TOWN], BF16) for hp in range(4)] for br in range(2)]
        att_mark = A.mark()
        self.phase_att(Qreg, oT, maskb, hm1b, KMd, KRd, VMd, KFd, VFd, FQd, QFd)
        P.barrier()
        A.reset(att_mark)
        if self.stop_after == "ATT":
            return self.finish_dbg({"oT%d_%d" % (br, hp): (oT[br][hp], [128, TOWN], BF16) for br in range(2) for hp in (0, 3)})

        yT = A.alloc_top("yT", [128, 8, TOWN], BF16)
        self.phase_merge1(oT, yT, HTd, w_om, w_of, w_gm, w_gf)
        P.barrier()
        A.reset(persist_mark)
        if self.stop_after == "M1":
            return self.finish_dbg({"yT": (yT, [128, 8, TOWN], BF16)})
        xT = A.alloc("xT", [128, 8, TOWN], F32)
        x_mark = A.mark()
        self.phase_merge2(xT, yT, x_own, w_out, gm)
        P.barrier()
        A.reset(x_mark)
        A.top = SB_LIMIT
        if self.stop_after == "M2":
            return self.finish_dbg({"xT": (xT, [128, 8, TOWN], F32)})
        self.phase_ffn(xT, w_up, w_down, conv_w_c, conv_b_c, g_fin_r, gs2, sh2, gf, hval, out_d)
        if self.stop_after == "F1":
            return self.finish_dbg({"h2T": (self.h2T_dbg, [128, 8, TOWN], BF16), "xT": (xT, [128, 8, TOWN], F32)})
        P.barrier()
        P.emit()
        return nc

    def rope_tables(self, eng, posi, ncol, cc, ss, ti, tf, a, key, sink=None, okey=None):
        P = self.P
        add = sink or P.add
        okey = okey or key
        invf, sgn = self.invf_t, self.sgn_t
        r = slice(64, 96)
        inv2pi = float(np.float32(1.0 / (2 * np.pi)))
        c1 = float(np.float32(6.28125))
        c2 = float(np.float32(2 * np.pi - 6.28125))
        c3 = float(2 * np.pi - np.float64(np.float32(6.28125)) - np.float64(np.float32(2 * np.pi - 6.28125)))
        pi = float(np.pi)
        k = key
        add(eng, lambda e: e.tensor_copy(out=tf, in_=posi), reads=[okey + "posi"], writes=[k + "tf"])
        add(eng, lambda e: e.tensor_scalar(out=a, in0=tf, scalar1=invf[r, 0:1], scalar2=None, op0=ALU.mult),
              reads=[k + "tf", "invf"], writes=[k + "ang"])
        add(eng, lambda e: e.tensor_scalar(out=ti, in0=a, scalar1=inv2pi, scalar2=None, op0=ALU.mult),
              reads=[k + "ang"], writes=[k + "ti"])
        add(eng, lambda e: e.tensor_copy(out=tf, in_=ti), reads=[k + "ti"], writes=[k + "tf"])
        for cc_ in (c1, c2, c3):
            add(eng, lambda e, cc_=cc_: e.scalar_tensor_tensor(out=a, in0=tf, scalar=-cc_, in1=a, op0=ALU.mult, op1=ALU.add),
                  reads=[k + "tf", k + "ang"], writes=[k + "ang"])
        add(eng, lambda e: e.tensor_scalar(out=tf, in0=a, scalar1=pi, scalar2=-2 * pi, op0=ALU.is_gt, op1=ALU.mult),
              reads=[k + "ang"], writes=[k + "tf"])
        add(eng, lambda e: e.tensor_tensor(out=a, in0=a, in1=tf, op=ALU.add), reads=[k + "ang", k + "tf"], writes=[k + "ang"])
        add(eng, lambda e: e.tensor_scalar(out=tf, in0=a, scalar1=-pi, scalar2=2 * pi, op0=ALU.is_lt, op1=ALU.mult),
              reads=[k + "ang"], writes=[k + "tf"])
        add(eng, lambda e: e.tensor_tensor(out=a, in0=a, in1=tf, op=ALU.add), reads=[k + "ang", k + "tf"], writes=[k + "ang"])
        add("act", lambda e: e.activation(out=ss, in_=a, func=AF.Sin, scale=sgn[r, 0:1]),
              reads=[k + "ang", "sgn"], writes=[okey + "ss"])
        add(eng, lambda e: e.tensor_scalar_add(out=a, in0=a, scalar1=pi / 2), reads=[k + "ang", k + "ss"], writes=[k + "ang"])
        add(eng, lambda e: e.tensor_scalar(out=tf, in0=a, scalar1=pi, scalar2=-2 * pi, op0=ALU.is_gt, op1=ALU.mult),
              reads=[k + "ang"], writes=[k + "tf"])
        add(eng, lambda e: e.tensor_tensor(out=a, in0=a, in1=tf, op=ALU.add), reads=[k + "ang", k + "tf"], writes=[k + "ang"])
        add("act", lambda e: e.activation(out=cc, in_=a, func=AF.Sin), reads=[k + "ang"], writes=[okey + "cc"])

    def nt_stats(self, tiles, nf, key, scr, sink=None, scale_sink=None):
        P = self.P
        sink = sink or P.add
        scale_sink = scale_sink or sink
        nt = len(tiles)
        ssq, rv, rstd, junk = scr["ssq"], scr["rv"], scr["rstd"], scr["junk"]
        F = nf * 128
        for j, (src, rows, skeys, is_psum, tmp) in enumerate(tiles):
            sink("act", lambda e, j=j, src=src, rows=rows: e.activation(out=junk[0:rows, 0:F], in_=src, func=AF.Square, accum_out=ssq[0:rows, j:j + 1]),
                 reads=list(skeys), writes=[key + "ssq%d" % j, key + "junk"])
        rmax = max(t[1] for t in tiles)
        sink("dve", lambda e: e.tensor_scalar(out=rv[0:rmax, 0:nt], in0=ssq[0:rmax, 0:nt], scalar1=1.0 / F, scalar2=EPS, op0=ALU.mult, op1=ALU.add),
             reads=[key + "ssq%d" % j for j in range(nt)], writes=[key + "rv"])
        sink("pool", lambda e: e.tensor_tensor(out=rstd[0:rmax, 0:nt], in0=rv[0:rmax, 0:nt], in1=self.mhalf[0:rmax, 0:nt], op=ALU.pow),
             reads=[key + "rv", "mhalf"], writes=[key + "rstd"])
        srcs = []
        for j, (src, rows, skeys, is_psum, tmp) in enumerate(tiles):
            dst = tmp if is_psum else src
            wk = [key + "xn%d" % j] if is_psum else list(skeys)
            scale_sink("dve", lambda e, j=j, src=src, dst=dst, rows=rows: e.tensor_scalar(out=dst, in0=src, scalar1=rstd[0:rows, j:j + 1], scalar2=None, op0=ALU.mult),
                       reads=list(skeys) + [key + "rstd"], writes=wk)
            srcs.append((dst, rows, wk))
        return srcs

    def nt_trans(self, srcs, nf, dst_fn, gain, bias):
        P, B = self.P, self.B
        ident_f = self.consts["ident_f"]
        for f in range(nf):
            bk = f % 2

            def fn(e, f=f, bk=bk):
                inst = None
                col = 0
                for (dst, rows, wk) in srcs:
                    inst = e.transpose(out=B[bk][:, col:col + rows], in_=dst[:, f * 128:(f + 1) * 128], identity=ident_f[0:rows, 0:rows])
                    col += rows
                return inst
            rk = []
            for (_, _, wk) in srcs:
                rk += wk
            P.add("pe", fn, reads=rk + ["ident_f"], writes=["B%d" % bk])
            ncol = sum(s_[1] for s_ in srcs)
            out_ap, okeys = dst_fn(f, ncol)
            self.evac(self.evac_engine(), out_ap, B[bk][:, 0:ncol], scale=gain[:, f:f + 1],
                      bias=(bias[:, f:f + 1] if bias is not None else None), reads=["B%d" % bk], writes=okeys)

    def norm_transpose(self, tiles, nf, dst_fn, gain, bias, key, scr):
        srcs = self.nt_stats(tiles, nf, key, scr)
        self.nt_trans(srcs, nf, dst_fn, gain, bias)

    def phase_kv(self, x_kv, pos_kv, w_ckv, w_kr, w_krs, w_fk, w_fv, w_fl, w_ukv, b_fg_c, cmat_d,
                 gs1, sh1, gkv, invf, sgn, flg, KMd, KRd, VMd, KFd, VFd, FQd):
        nc, P, A, B = self.nc, self.P, self.A, self.B
        self.invf_t, self.sgn_t = invf, sgn
        wckv = A.alloc("wckv", [128, 8, 256], BF16)
        wkr = A.alloc("wkr", [128, 8, 96], BF16)
        wkrs = A.alloc("wkrs", [128, 8, 96], BF16)
        wfk = A.alloc("wfk", [128, 8, 512], BF16)
        wfv = A.alloc("wfv", [128, 8, 512], BF16)
        wfl = A.alloc("wfl", [128, 8, 8], BF16)
        wukv = A.alloc("wukv", [128, 2, 1024], BF16)
        for t, d, nm in ((wckv, w_ckv, "wckv"), (wkr, w_kr, "wkr"), (wkrs, w_krs, "wkrs"), (wfk, w_fk, "wfk"),
                         (wfv, w_fv, "wfv"), (wfl, w_fl, "wfl"), (wukv, w_ukv, "wukv")):
            self.dma("pool", t[:], wview(d), writes=[nm])
        xt = [A.alloc("xt%d" % i, [128, 4, D], F32) for i in range(2)]
        hT = [A.alloc("hT%d" % i, [128, 8, 512], BF16) for i in range(2)]
        junk = A.alloc("junk", [128, D], BF16)
        scr = dict(ssq=A.alloc("ssq", [128, 4], F32), rv=A.alloc("rv", [128, 4], F32), rstd=A.alloc("rstd", [128, 4], F32), junk=junk)
        junk2 = A.alloc("junk2", [128, 256], BF16)
        scr2 = dict(ssq=A.alloc("ssq2", [128, 4], F32), rv=A.alloc("rv2", [128, 4], F32), rstd=A.alloc("rstd2", [128, 4], F32), junk=junk2)
        latn = A.alloc("latn", [128, 4, 256], F32)
        ckvT = A.alloc("ckvT", [128, 2, 512], BF16)
        stgM = [A.alloc("stgM%d" % i, [128, 8, 512], BF16) for i in range(2)]
        stgK = [A.alloc("stgK%d" % i, [128, 4, 512], BF16) for i in range(2)]
        stgV = [A.alloc("stgV%d" % i, [128, 4, 512], BF16) for i in range(2)]
        krst = [A.alloc("krst%d" % i, [96, 512], BF16) for i in range(2)]
        posi = A.alloc("posi", [96, 512], I32)
        tmp_i = A.alloc("tmp_i", [96, 512], I32)
        tmp_f = A.alloc("tmp_f", [96, 512], F32)
        ang = A.alloc("ang", [96, 512], F32)
        ccK = A.alloc("ccK", [96, 512], F32)
        ssK = A.alloc("ssK", [96, 512], F32)
        rt1 = A.alloc("rt1", [96, 512], F32)
        rt2 = A.alloc("rt2", [96, 512], F32)
        LFraw = A.alloc("LFraw", [128, 512], F32)
        lfst = [A.alloc("lfst%d" % i, [8, 512], F32) for i in range(2)]

        for nm, d in (("wcq", self.din["w_cq"]), ("wfq", self.din["w_fq"]), ("wuq", self.din["w_uq"]), ("wuqs", self.din["w_uqs"])):
            self.dma("pool", self.qw[nm][:], wview(d), writes=["q_" + nm])
        ac = self.ada_ctx
        wah = A.alloc("wah", [128, 8, 512], F32)
        posi2 = [posi, A.alloc("posi_b", [96, 512], I32)]
        ccK2 = [ccK, A.alloc("ccK_b", [96, 512], F32)]
        ssK2 = [ssK, A.alloc("ssK_b", [96, 512], F32)]
        rr = slice(64, 96)

        class Lazy:
            def __init__(self):
                self.q = []

            def add(self, eng, fn, reads=(), writes=()):
                self.q.append((eng, fn, list(reads), list(writes)))

            def emit(self, n):
                for _ in range(min(n, len(self.q))):
                    eng, fn, r, w = self.q.pop(0)
                    P.add(eng, fn, reads=r, writes=w)

            def flush(self):
                self.emit(len(self.q))

        def prep(c, lazy):
            xb = c % 2
            tok = slice(c * 512, (c + 1) * 512)
            self.dma("sp", xt[xb][:], x_kv[tok, :].rearrange("(j p) d -> p j d", p=128), writes=["xt%d" % xb])
            self.dma("sp", posi2[xb][64:96, :], pos_kv[0:1, tok].partition_broadcast(32), writes=["K%dposi" % xb])
            tiles = [(xt[xb][:, j, :], 128, ["xt%d" % xb], False, None) for j in range(4)]
            tail = Lazy()
            srcs = self.nt_stats(tiles, 8, "kx", scr, sink=lazy.add, scale_sink=tail.add)
            self.rope_tables("dve", posi2[xb][rr, :], 512, ccK2[xb][rr, :], ssK2[xb][rr, :], tmp_i[rr, :], tmp_f[rr, :], ang[rr, :], "K",
                             sink=lazy.add, okey="K%d" % xb)
            lazy.q.extend(tail.q)
            return srcs

        def trans(c, srcs):
            hb = c % 2

            def dst_fn(f, ncol, hb=hb):
                return hT[hb][:, f, 0:ncol], ["hT%d_%d" % (hb, f)]
            self.nt_trans(srcs, 8, dst_fn, gs1, sh1)

        lz = Lazy()
        srcs_next = prep(0, lz)
        lz.flush()
        trans(0, srcs_next)
        for c in range(16):
            xb = c % 2
            hb = c % 2
            sb = c % 2
            tok = slice(c * 512, (c + 1) * 512)
            hkeys = ["hT%d_%d" % (hb, f) for f in range(8)]
            for j in range(4):
                bk = 2 + j // 2
                self.mm_group(B[bk][:, (j % 2) * 256:(j % 2) * 256 + 256],
                              [(hT[hb][:, k, j * 128:(j + 1) * 128], wckv[:, k, :]) for k in range(8)],
                              reads=hkeys + ["wckv"], writes=["B%d" % bk])
            ltiles = [(B[2 + j // 2][:, (j % 2) * 256:(j % 2) * 256 + 256], 128, ["B%d" % (2 + j // 2)], True, latn[:, j, :]) for j in range(4)]
            lsrcs = self.nt_stats(ltiles, 2, "kl", scr2)
            lz = Lazy()
            if c + 1 < 16:
                srcs_next = prep(c + 1, lz)
            bk1 = self.bank_rot()
            self.mm_group(B[bk1][0:96, 0:512], [(wkr[:, k, :], hT[hb][:, k, :]) for k in range(8)], reads=hkeys + ["wkr"], writes=["B%d" % bk1])
            bk2 = self.bank_rot()
            self.mm_group(B[bk2][0:96, 0:512], [(wkrs[:, k, :], hT[hb][:, k, :]) for k in range(8)], reads=hkeys + ["wkrs"], writes=["B%d" % bk2])
            P.add("dve", lambda e, bk1=bk1, xb=xb: e.tensor_tensor(out=rt1[rr, :], in0=B[bk1][rr, 0:512], in1=ccK2[xb][rr, :], op=ALU.mult),
                  reads=["B%d" % bk1, "K%dcc" % xb], writes=["rt1"])
            P.add("dve", lambda e, bk2=bk2, xb=xb: e.tensor_tensor(out=rt2[rr, :], in0=B[bk2][rr, 0:512], in1=ssK2[xb][rr, :], op=ALU.mult),
                  reads=["B%d" % bk2, "K%dss" % xb], writes=["rt2"])
            P.add("dve", lambda e, sb=sb: e.tensor_tensor(out=krst[sb][rr, :], in0=rt1[rr, :], in1=rt2[rr, :], op=ALU.add),
                  reads=["rt1", "rt2"], writes=["krst%d" % sb])
            self.dma("sp", KRd[:, tok], krst[sb][rr, :], reads=["krst%d" % sb], writes=["KRd"])
            self.evac("act", lfst[sb][:], B[bk1][0:8, 0:512], reads=["B%d" % bk1], writes=["lfst%d" % sb])
            self.dma("sp", LFraw[c * 8:(c + 1) * 8, :], lfst[sb][:], reads=["lfst%d" % sb], writes=["LFraw%d" % c])
            lz.emit(3)
            for (wt, stg, nm, dd, wkey) in ((wfk, stgK, "stgK", KFd, "wfk"), (wfv, stgV, "stgV", VFd, "wfv")):
                for hp in range(4):
                    bk = self.bank_rot()
                    self.mm_group(B[bk][:, 0:512], [(wt[:, k, hp * 128:(hp + 1) * 128], hT[hb][:, k, :]) for k in range(8)],
                                  reads=hkeys + [wkey], writes=["B%d" % bk])
                    self.evac(self.evac_engine(), stg[sb][:, hp, :], B[bk][:, 0:512], reads=["B%d" % bk], writes=["%s%d_%d" % (nm, sb, hp)])
                    lz.emit(3)
                sk = ["%s%d_%d" % (nm, sb, hp) for hp in range(4)]
                for two in range(2):
                    self.dma("sp", dd[two::2, 0:64, tok].rearrange("h r t -> r h t"), stg[sb][two * 64:(two + 1) * 64, :, :],
                             reads=sk, writes=[nm + "d"])
            lz.flush()
            if 1 <= c <= 8:
                hp_ = c - 1
                col0 = 2 * D + hp_ * 512
                self.dma("pool", wah[:], wview(ac["w_ada"][:, col0:col0 + 512]), writes=["wah"])
                bka = self.bank_rot()

                def fn(e, bka=bka):
                    inst = None
                    for j in range(4):
                        for k in range(8):
                            inst = e.matmul(B[bka][:, j:j + 1], lhsT=wah[:, k, j * 128:(j + 1) * 128], rhs=ac["cact"][:, k:k + 1], start=(k == 0), stop=(k == 7))
                    return inst
                P.add("pe", fn, reads=["wah"], writes=["B%d" % bka])
                a0_ = 16 + 4 * hp_
                P.add("dve", lambda e, bka=bka, a0_=a0_: e.tensor_tensor(out=ac["ada"][:, a0_:a0_ + 4], in0=B[bka][:, 0:4], in1=ac["bada"][:, a0_:a0_ + 4], op=ALU.add),
                      reads=["B%d" % bka], writes=["ada_h%d" % hp_])
            if c + 1 < 16:
                trans(c + 1, srcs_next)
            def dst_fn2(f, ncol):
                return ckvT[:, f, 0:ncol], ["ckvT%d" % f]
            self.nt_trans(lsrcs, 2, dst_fn2, gkv, None)
            for h in range(8):
                bk = self.bank_rot()
                self.mm_group(B[bk][:, 0:512], [(wukv[:, k2, h * 128:(h + 1) * 128], ckvT[:, k2, :]) for k2 in range(2)],
                              reads=["ckvT0", "ckvT1", "wukv"], writes=["B%d" % bk])
                self.evac(self.evac_engine(), stgM[sb][:, h, :], B[bk][:, 0:512], reads=["B%d" % bk], writes=["stgM%d_%d" % (sb, h)])
            mk = ["stgM%d_%d" % (sb, h) for h in range(8)]
            self.dma("sp", KMd[:, :, tok].rearrange("h r t -> r h t"), stgM[sb][0:64, :, :], reads=mk, writes=["KMd"])
            self.dma("sp", VMd[:, :, tok].rearrange("h r t -> r h t"), stgM[sb][64:128, :, :], reads=mk, writes=["VMd"])
        tmp8b = A.alloc("tmp8b", [128, 8], F32)
        P.add("dve", lambda e: e.tensor_scalar_add(out=tmp8b[:], in0=ac["ada"][:, 32:40], scalar1=1.0), reads=["ada_h%d" % i for i in range(8)], writes=["tmp8b"])
        P.add("dve", lambda e: e.tensor_tensor(out=ac["gs2"][:], in0=tmp8b[:], in1=ac["gffn"][:], op=ALU.mult), reads=["tmp8b"], writes=["gs2"])

        bfg = A.alloc("bfg", [128, 1], F32)
        cmat = A.alloc("cmat", [128, 128], F32)
        nlf = A.alloc("nlf", [128, 512], F32)
        Fx = A.alloc("Fx", [128, 514], F32)
        carry = A.alloc("carry", [128, 1], F32)
        rem = A.alloc("rem", [128, 512], F32)
        parts = [A.alloc("part%d" % j, [128, 512], BF16) for j in range(3)]
        fqn = A.alloc("fqn", [128, SBW], F32)
        qparts = [A.alloc("qpart%d" % j, [128, SBW], BF16) for j in range(3)]
        self.dma("sp", bfg[:], b_fg_c, writes=["bfg"])
        self.dma("sp", cmat[:], cmat_d, writes=["cmat"])
        P.add("dve", lambda e: e.tensor_scalar(out=bfg[:], in0=bfg[:], scalar1=-1.0, scalar2=None, op0=ALU.mult), reads=["bfg"], writes=["bfg"])
        lk = ["LFraw%d" % c for c in range(16)]
        P.add("act", lambda e: e.activation(out=nlf[:], in_=LFraw[:], func=AF.Exp, scale=-1.0, bias=bfg[:, 0:1]), reads=lk + ["bfg"], writes=["nlf"])
        P.add("act", lambda e: e.activation(out=nlf[:], in_=nlf[:], func=AF.Ln, bias=1.0), reads=["nlf"], writes=["nlf"])
        ones_f = self.consts["ones_f"]
        P.add("dve", lambda e: e.tensor_tensor_scan(out=Fx[:, 2:514], data0=ones_f[:, 0:512], data1=nlf[:], initial=0.0, op0=ALU.mult, op1=ALU.add),
              reads=["nlf", "ones_f"], writes=["Fx"])
        P.add("pe", lambda e: e.matmul(B[4][:, 0:1], lhsT=cmat[:], rhs=Fx[:, 513:514], start=True, stop=True), reads=["cmat", "Fx"], writes=["B4"])
        P.add("dve", lambda e: e.tensor_copy(out=carry[:], in_=B[4][:, 0:1]), reads=["B4"], writes=["carry"])
        P.add("dve", lambda e: e.tensor_scalar(out=Fx[:, 2:514], in0=Fx[:, 2:514], scalar1=carry[:, 0:1], scalar2=None, op0=ALU.add),
              reads=["Fx", "carry"], writes=["Fx"])
        P.add("pool", lambda e: e.memset(Fx[0:8, 0:2], 0.0), writes=["Fxh0"])
        self.dma("sp", Fx[8:128, 0:2], Fx[0:120, 512:514], reads=["Fx"], writes=["Fxh"])

        def split3(src, tgt, rm, k, n, skeys):
            P.add("dve", lambda e: e.tensor_copy(out=tgt[0][:, 0:n], in_=src), reads=skeys, writes=[k + "p0"])
            P.add("dve", lambda e: e.tensor_tensor(out=rm[:, 0:n], in0=src, in1=tgt[0][:, 0:n], op=ALU.subtract), reads=skeys + [k + "p0"], writes=["rem"])
            P.add("dve", lambda e: e.tensor_copy(out=tgt[1][:, 0:n], in_=rm[:, 0:n]), reads=["rem"], writes=[k + "p1"])
            P.add("dve", lambda e: e.tensor_tensor(out=rm[:, 0:n], in0=rm[:, 0:n], in1=tgt[1][:, 0:n], op=ALU.subtract), reads=["rem", k + "p1"], writes=["rem"])
            P.add("dve", lambda e: e.tensor_copy(out=tgt[2][:, 0:n], in_=rm[:, 0:n]), reads=["rem"], writes=[k + "p2"])
        split3(Fx[:, 2:514], parts, rem, "K", 512, ["Fx"])
        for j in range(3):
            if os.environ.get("F_BATCH", "1") == "1":
                self.dma("sp", KFd[:, 64 + j, :].rearrange("h (c t) -> c h t", c=16), parts[j][:],
                         reads=["Kp%d" % j], writes=["KFd_f"])
            else:
                for c in range(16):
                    self.dma("sp" if (c % 2 == 0) else "pool", KFd[:, 64 + j, c * 512:(c + 1) * 512], parts[j][c * 8:(c + 1) * 8, :],
                             reads=["Kp%d" % j], writes=["KFd_f"])
        P.add("dve", lambda e: e.tensor_scalar(out=fqn[:], in0=Fx[:, 0:SBW], scalar1=flg[:, 0:1], scalar2=None, op0=ALU.mult),
              reads=["Fx", "Fxh", "Fxh0", "flg"], writes=["fqn"])
        for d in range(1, 4):
            P.add("dve", lambda e, d=d: e.scalar_tensor_tensor(out=fqn[:], in0=Fx[:, 128 * d:128 * d + SBW], scalar=flg[:, d:d + 1], in1=fqn[:], op0=ALU.mult, op1=ALU.add),
                  reads=["Fx", "Fxh", "fqn", "flg"], writes=["fqn"])
        P.add("dve", lambda e: e.tensor_scalar(out=fqn[:], in0=fqn[:], scalar1=-1.0, scalar2=None, op0=ALU.mult), reads=["fqn"], writes=["fqn"])
        split3(fqn[:], qparts, rem, "Q", SBW, ["fqn"])
        for j in range(3):
            self.dma("sp", FQd[j], qparts[j][:], reads=["Qp%d" % j], writes=["FQd"])

    def phase_q(self, x_own, pos_own, w_cq, w_fq, w_uq, w_uqs, gs1, sh1, gq, invf, sgn, Qreg, HTd, QFd):
        nc, P, A, B = self.nc, self.P, self.A, self.B
        self.invf_t, self.sgn_t = invf, sgn
        wcq, wfq, wuq, wuqs = self.qw["wcq"], self.qw["wfq"], self.qw["wuq"], self.qw["wuqs"]
        hTo = A.alloc("hTo", [128, 8, TOWN], BF16)
        cqT = A.alloc("cqT", [128, 3, TOWN], BF16)
        xt = [A.alloc("xq%d" % i, [128, 4, D], F32) for i in range(2)]
        junk = A.alloc("junkq", [128, D], BF16)
        scr = dict(ssq=A.alloc("ssqq", [128, 4], F32), rv=A.alloc("rvq", [128, 4], F32), rstd=A.alloc("rstdq", [128, 4], F32), junk=junk)
        junkq2 = A.alloc("junkq2", [128, 384], BF16)
        scr2 = dict(ssq=A.alloc("ssqq2", [128, 4], F32), rv=A.alloc("rvq2", [128, 4], F32), rstd=A.alloc("rstdq2", [128, 4], F32), junk=junkq2)
        latq = A.alloc("latq", [128, 4, 384], F32)
        ccQ = A.alloc("ccQ", [96, TOWN], F32)
        ssQ = A.alloc("ssQ", [96, TOWN], F32)
        posi = A.alloc("posiq", [96, TOWN], I32)
        tmp_i = A.alloc("tmp_iq", [96, 520], I32)
        tmp_f = A.alloc("tmp_fq", [96, 520], F32)
        ang = A.alloc("angq", [96, 520], F32)
        rt1 = A.alloc("rt1q", [96, QT], F32)
        rt2 = A.alloc("rt2q", [96, QT], F32)
        qfst = [A.alloc("qfst%d" % i, [64, TOWN], BF16) for i in range(2)]
        rr = slice(64, 96)
        for h in range(8):
            P.add("pool", lambda e, h=h: e.memset(Qreg[h][64:128, :], 0.0), writes=["Qr%d_%d" % (h, n) for n in range(NQT)])
        self.dma("sp", posi[rr, :], pos_own[0:1, :].partition_broadcast(32), writes=["Qposi"])
        for q4 in range(4):
            cs = slice(q4 * 520, (q4 + 1) * 520)
            self.rope_tables("dve", posi[rr, cs], 520, ccQ[rr, cs], ssQ[rr, cs], tmp_i[rr, :], tmp_f[rr, :], ang[rr, :], "Q")
        chunks = [(c * 512, 4, 128) for c in range(4)] + [(2048, 1, 32)]
        for ci, (t0, ntile, rows) in enumerate(chunks):
            xb = ci % 2
            if rows == 128:
                self.dma("sp", xt[xb][:], x_own[t0:t0 + 512, :].rearrange("(j p) d -> p j d", p=128), writes=["xq%d" % xb])
            else:
                self.dma("sp", xt[xb][0:32, 0, :], x_own[t0:t0 + 32, :], writes=["xq%d" % xb])
            tiles = [(xt[xb][0:rows, j, :], rows, ["xq%d" % xb], False, None) for j in range(ntile)]

            def dst_fn(f, ncol, t0=t0):
                return hTo[:, f, t0:t0 + ncol], ["hTo%d_%d" % (f, t0)]
            self.norm_transpose(tiles, 8, dst_fn, gs1, sh1, "qx", scr)
        hk_all = ["hTo%d_%d" % (f, t0) for f in range(8) for (t0, _, _) in chunks]
        self.dma("sp", HTd, hTo[:], reads=hk_all, writes=["HTd"])
        def fq_head(h):
            sb = h % 2
            for n in range(NQT):
                cs = slice(n * QT, (n + 1) * QT)
                bk = self.bank_rot(lo=6, n=2)
                self.mm_group(B[bk][0:64, 0:QT], [(wfq[:, k, h * 64:(h + 1) * 64], hTo[:, k, cs]) for k in range(8)],
                              reads=hk_all + ["wfq"], writes=["B%d" % bk])
                self.evac(self.evac_engine(), qfst[sb][:, cs], B[bk][0:64, 0:QT], scale=0.125, reads=["B%d" % bk], writes=["qfst%d_%d" % (sb, n)])
            self.dma("sp", QFd[h], qfst[sb][:], reads=["qfst%d_%d" % (sb, n) for n in range(NQT)], writes=["QFd%d" % h])
        fq_plan = [[0, 1], [2, 3], [4, 5], [6], [7]]
        for ci, (t0, ntile, rows) in enumerate(chunks):
            hk = ["hTo%d_%d" % (f, t0) for f in range(8)]
            for j in range(ntile):
                bk = 2 + j
                self.mm_group(B[bk][0:rows, 0:384], [(hTo[:, k, t0 + j * 128:t0 + j * 128 + rows], wcq[:, k, :]) for k in range(8)],
                              reads=hk + ["wcq"], writes=["B%d" % bk])
            ltiles = [(B[2 + j][0:rows, 0:384], rows, ["B%d" % (2 + j)], True, latq[0:rows, j, :]) for j in range(ntile)]

            def dst_fn2(f, ncol, t0=t0):
                return cqT[:, f, t0:t0 + ncol], ["cqT%d_%d" % (f, t0)]
            lsrcs = self.nt_stats(ltiles, 3, "ql", scr2)
            for h in fq_plan[ci]:
                fq_head(h)
            self.nt_trans(lsrcs, 3, dst_fn2, gq, None)
        ck_all = ["cqT%d_%d" % (f, t0) for f in range(3) for (t0, _, _) in chunks]
        for h in range(8):
            for n in range(NQT):
                cs = slice(n * QT, (n + 1) * QT)
                b1 = self.bank_rot()
                self.mm_group(B[b1][0:96, 0:QT], [(wuq[:, k, h * 96:(h + 1) * 96], cqT[:, k, cs]) for k in range(3)],
                              reads=ck_all + ["wuq"], writes=["B%d" % b1])
                b2 = self.bank_rot()
                self.mm_group(B[b2][0:96, 0:QT], [(wuqs[:, k, h * 96:(h + 1) * 96], cqT[:, k, cs]) for k in range(3)],
                              reads=ck_all + ["wuqs"], writes=["B%d" % b2])
                self.evac(self.evac_engine(), Qreg[h][0:64, cs], B[b1][0:64, 0:QT], reads=["B%d" % b1], writes=["Q%d_%d" % (h, n)])
                P.add("dve", lambda e, b1=b1, cs=cs: e.tensor_tensor(out=rt1[rr, :], in0=B[b1][rr, 0:QT], in1=ccQ[rr, cs], op=ALU.mult),
                      reads=["B%d" % b1, "Qcc"], writes=["rt1q"])
                P.add("dve", lambda e, b2=b2, cs=cs: e.tensor_tensor(out=rt2[rr, :], in0=B[b2][rr, 0:QT], in1=ssQ[rr, cs], op=ALU.mult),
                      reads=["B%d" % b2, "Qss"], writes=["rt2q"])
                P.add("dve", lambda e, h=h, cs=cs: e.tensor_tensor(out=Qreg[h][rr, cs], in0=rt1[rr, :], in1=rt2[rr, :], op=ALU.add),
                      reads=["rt1q", "rt2q"], writes=["Qr%d_%d" % (h, n)])

    def phase_att(self, Qreg, oT, maskb, hm1b, KMd, KRd, VMd, KFd, VFd, FQd, QFd):
        nc, P, A, B = self.nc, self.P, self.A, self.B
        psS = self.psS
        ident_b, ones_f = self.consts["ident_b"], self.consts["ones_f"]
        Kb = [A.alloc("Kb%d" % i, [128, S], BF16) for i in range(2)]
        VTb = A.alloc("VTb", [64, S], BF16)
        Va = [A.alloc("Va%d" % i, [128, 64, 128], BF16) for i in range(2)]
        Pt = [A.alloc("Pt%d" % i, [128, 2, QT], BF16) for i in range(3)]
        osb = [A.alloc("osb%d" % i, [128, QT], F32) for i in range(2)]
        self.osb = osb
        rden = A.alloc("rden", [128, 2 * QT], F32)
        bc_sb = A.alloc("bc_sb", [64, QT], F32)
        Bb7 = self.psB_raw[1][:].bitcast(BF16)
        for i in range(2):
            P.add("pool", lambda e, i=i: e.memset(Va[i][:, :, (64 if i == 0 else 0):(128 if i == 0 else 64)], 1.0), writes=["Va%d_ones" % i])
            P.add("pool", lambda e, i=i: e.memset(Kb[i][64:128, :], 0.0), writes=["Kb%d" % i])
        unit = 0
        hcount = 0
        pre_hook = None
        for br in range(2):
            KR = 128
            scale = MLA_SCALE if br == 0 else 1.0
            def make_switch(h):
                def sw():
                    qk = ["Q%d_%d" % (h, n) for n in range(NQT)] + ["Qr%d_%d" % (h, n) for n in range(NQT)]
                    P.add("pool", lambda e: e.memset(Qreg[h][64:128, :], 0.0), reads=[], writes=qk)
                    P.add("pool", lambda e: e.memset(Qreg[h][64:70, :], 1.0), reads=[], writes=qk)
                    self.dma("sp", Qreg[h][0:64, :], QFd[h], writes=qk)
                    for j in range(3):
                        self.dma("sp", Qreg[h][64 + j:65 + j, :].rearrange("o (m i) -> o m i", i=SBW),
                                 FQd[j].rearrange("(m h) i -> h m i", h=8)[h:h + 1], writes=qk)
                return sw
            switch = [make_switch(h) for h in range(8)] if br == 0 else None
            if br == 1:
                for i in range(2):
                    P.add("pool", lambda e, i=i: e.memset(Kb[i][64:67, :], 1.0), writes=["Kb%d" % i])
            pairs = []
            loads = []
            vts = []
            for h in range(8):
                hb = hcount % 2
                hcount += 1
                qk = ["Q%d_%d" % (h, n) for n in range(NQT)] + ["Qr%d_%d" % (h, n) for n in range(NQT)]
                def load(h=h, hb=hb, br=br):
                    if br == 0:
                        self.dma("sp", Kb[hb][0:64, :], KMd[h], writes=["Kb%d" % hb])
                        self.dma("sp", Kb[hb][64:96, :], KRd, writes=["Kb%d" % hb])
                        self.dma("sp", VTb[:], VMd[h], writes=["VTb"])
                    else:
                        self.dma("sp", Kb[hb][0:64, :], KFd[h, 0:64, :], writes=["Kb%d" % hb])
                        self.dma("sp", Kb[hb][67:70, :], KFd[h, 64:67, :], writes=["Kb%d" % hb])
                        self.dma("sp", VTb[:], VFd[h], writes=["VTb"])

                def vtrans(hb=hb):
                    for rnd in range(4):
                        def fn(e, rnd=rnd):
                            inst = None
                            for i in range(16):
                                t = rnd * 16 + i
                                inst = e.transpose(out=Bb7[:, i * 64:(i + 1) * 64], in_=VTb[0:64, t * 128:(t + 1) * 128], identity=ident_b[0:64, 0:64])
                            return inst
                        P.add("pe", fn, reads=["VTb", "ident_b"], writes=["B7"])
                        P.add("dve", lambda e, rnd=rnd, hb=hb: e.tensor_copy(out=Va[hb][:, rnd * 16:(rnd + 1) * 16, (0 if hb == 0 else 64):(64 if hb == 0 else 128)],
                                                                              in_=Bb7.rearrange("p (t d) -> p t d", d=64)),
                              reads=["B7"], writes=["Va%d" % hb])
                loads.append(load)
                vts.append(vtrans)
                for Mq in range(NQT):
                    ob = 6
                    ou = unit % 2
                    unit += 1
                    npair = (8 * Mq + 8) // 2
                    for p in range(npair):
                        pairs.append(dict(h=h, hb=hb, Mq=Mq, p=p, ob=ob, ou=ou, last=(p == npair - 1), KR=KR, scale=scale, br=br, qk=qk,
                                          first=(Mq == 0 and p == 0)))
            self.att_stream(pairs, loads, vts, Kb, Va, Pt, Qreg, oT, maskb, hm1b, rden, bc_sb, switch, pre_hook)
            pre_hook = switch[7] if switch is not None else None

    def att_stream(self, pairs, loads, vts, Kb, Va, Pt, Qreg, oT, maskb, hm1b, rden, bc_sb, switch=None, pre_hook=None):
        P, B, psS = self.P, self.B, self.psS
        ident_b, ones_f = self.consts["ident_b"], self.consts["ones_f"]

        loads[0]()
        if pre_hook is not None:
            pre_hook()

        def qk_op(g):
            d_ = pairs[g]
            Mq, p, hb, h, KR, qk = d_["Mq"], d_["p"], d_["hb"], d_["h"], d_["KR"], d_["qk"]
            if d_["first"]:
                vts[h]()
                if h + 1 < len(loads):
                    loads[h + 1]()
                if switch is not None and h >= 1:
                    switch[h - 1]()
            q0 = Mq * QT
            sbuf = g % 3
            wide = (2 * p) < 8 * Mq + 4

            def fn(e):
                inst = None
                for j in range(2):
                    kb = 2 * p + j
                    lhsT = Kb[hb][0:KR, kb * 128:(kb + 1) * 128]
                    extra = []
                    if wide:
                        out = psS[sbuf][:, j, 0:QT]
                        rhs = Qreg[h][0:KR, q0:q0 + QT]
                        if kb == 8 * Mq - 1:
                            extra.append((psS[sbuf][:, j, 0:2], hm1b[:, 0:2]))
                        d = kb - 8 * Mq
                        if 0 <= d <= 3:
                            extra.append((psS[sbuf][:, j, 0:SBW], maskb[:, d, :]))
                            if d == 3:
                                extra.append((psS[sbuf][:, j, SBW:SBW + 2], hm1b[:, 0:2]))
                    else:
                        out = psS[sbuf][:, j, 0:SBW]
                        rhs = Qreg[h][0:KR, q0 + SBW:q0 + QT]
                        d = kb - 8 * Mq - 4
                        extra.append((psS[sbuf][:, j, 0:SBW], maskb[:, d, :]))
                    inst = e.matmul(out, lhsT=lhsT, rhs=rhs, start=True, stop=(len(extra) == 0))
                    for xi, (xo, xr) in enumerate(extra):
                        inst = e.matmul(xo, lhsT=ident_b[:], rhs=xr, start=False, stop=(xi == len(extra) - 1))
                return inst
            P.add("pe", fn, reads=["Kb%d" % hb, "maskb", "hm1b", "ident_b"] + qk, writes=["S%d" % sbuf])

        def exp_op(g):
            d_ = pairs[g]
            sbuf = g % 3
            W = QT if (2 * d_["p"]) < 8 * d_["Mq"] + 4 else SBW
            scale = d_["scale"]
            P.add("act", lambda e: e.activation(out=Pt[sbuf][:, :, 0:W], in_=psS[sbuf][:, :, 0:W], func=AF.Exp, scale=scale),
                  reads=["S%d" % sbuf], writes=["P%d" % sbuf])

        def pv_op(g):
            d_ = pairs[g]
            Mq, p, hb, ob = d_["Mq"], d_["p"], d_["hb"], d_["ob"]
            nkb = 8 * Mq + 8
            sbuf = g % 3
            wide = (2 * p) < 8 * Mq + 4

            def fn(e):
                inst = None
                for j in range(2):
                    kb = 2 * p + j
                    if wide:
                        out = B[ob][:, 0:QT]
                        rhs = Pt[sbuf][:, j, 0:QT]
                    else:
                        out = B[ob][:, SBW:QT]
                        rhs = Pt[sbuf][:, j, 0:SBW]
                    inst = e.matmul(out, lhsT=Va[hb][:, kb, :], rhs=rhs, start=(kb == 0), stop=(kb == nkb - 1))
                return inst
            P.add("pe", fn, reads=["P%d" % sbuf, "Va%d" % hb, "Va%d_ones" % hb], writes=["B%d" % ob])

        pending = []

        def end_unit(g):
            d_ = pairs[g]
            br, h, Mq, ou, ob = d_["br"], d_["h"], d_["Mq"], d_["ou"], d_["ob"]
            q0 = Mq * QT
            osb = self.osb
            hb = d_["hb"]
            dr = 64 if hb == 0 else 0
            vr = slice(0, 64) if hb == 0 else slice(64, 128)
            rd = rden[dr:dr + 1, ou * QT:(ou + 1) * QT]
            P.add("dve", lambda e: e.tensor_copy(out=osb[ou][:, :], in_=B[ob][:, 0:QT]), reads=["B%d" % ob], writes=["osb%d" % ou])
            P.add("dve", lambda e: e.tensor_scalar(out=rd, in0=osb[ou][dr:dr + 1, :], scalar1=1e-30, scalar2=None, op0=ALU.max),
                  reads=["osb%d" % ou], writes=["rden%d" % ou])
            P.add("dve", lambda e: e.reciprocal(out=rd, in_=rd), reads=["rden%d" % ou], writes=["rden%d" % ou])

            def tail():
                P.add("pe", lambda e: e.matmul(B[7][:, 0:QT], lhsT=ones_f[dr:dr + 1, 0:128], rhs=rd, start=True, stop=True),
                      reads=["rden%d" % ou, "ones_f"], writes=["B7"])
                P.add("dve", lambda e: e.tensor_tensor(out=oT[br][h // 2][vr, q0:q0 + QT], in0=osb[ou][vr, :], in1=B[7][vr, 0:QT], op=ALU.mult),
                      reads=["osb%d" % ou, "B7"], writes=["oT%d_%d_%d" % (br, h, Mq)])
            pending.append((g + 4, tail))

        def flush_pending(g):
            while pending and pending[0][0] <= g:
                pending.pop(0)[1]()

        n = len(pairs)
        import os
        nfill = int(os.environ.get("ATT_FILL", "0"))
        fillw = int(os.environ.get("ATT_FILLW", "128"))

        def filler(g):
            sbuf = g % 3
            for i in range(nfill):
                P.add("pe", lambda e, i=i: e.matmul(B[7][:, 0:fillw], lhsT=ident_b[:], rhs=maskb[:].rearrange("p a b -> p (a b)")[:, 0:fillw], start=True, stop=True),
                      reads=[], writes=[])
        for g in range(min(3, n)):
            qk_op(g)
        for g in range(n):
            exp_op(g)
            if g + 3 < n:
                qk_op(g + 3)
            pv_op(g)
            flush_pending(g)
            if pairs[g]["last"]:
                end_unit(g)
            filler(g)
        flush_pending(n + 100)

    def phase_merge1(self, oT, yT, HTd, w_om, w_of, w_gm, w_gf):
        nc, P, A, B = self.nc, self.P, self.A, self.B
        hTo = A.alloc("hTo2", [128, 8, TOWN], BF16)
        self.dma("sp", hTo[:], HTd, writes=["hTo2"])
        wom = [A.alloc("wom%d" % i, [128, 4, 128], BF16) for i in range(2)]
        wof = [A.alloc("wof%d" % i, [128, 4, 128], BF16) for i in range(2)]
        wgm = [A.alloc("wgm%d" % i, [128, 8, 128], BF16) for i in range(2)]
        wgf = [A.alloc("wgf%d" % i, [128, 8, 128], BF16) for i in range(2)]
        sgm = [A.alloc("sgm%d" % i, [128, QT], F32) for i in range(2)]
        sgf = [A.alloc("sgf%d" % i, [128, QT], F32) for i in range(2)]
        t1 = [A.alloc("t1_%d" % i, [128, QT], F32) for i in range(2)]
        t2 = [A.alloc("t2_%d" % i, [128, QT], F32) for i in range(2)]
        it = 0
        for dc in range(8):
            wb = dc % 2
            dcs = slice(dc * 128, (dc + 1) * 128)
            self.dma("pool", wom[wb][:], w_om[:, dcs].rearrange("(h r) c -> r h c", r=128), writes=["wom%d" % wb])
            self.dma("pool", wof[wb][:], w_of[:, dcs].rearrange("(h r) c -> r h c", r=128), writes=["wof%d" % wb])
            self.dma("pool", wgm[wb][:], wview(w_gm[:, dcs]), writes=["wgm%d" % wb])
            self.dma("pool", wgf[wb][:], wview(w_gf[:, dcs]), writes=["wgf%d" % wb])
            for n in range(NQT):
                cs = slice(n * QT, (n + 1) * QT)
                st = it % 2
                it += 1
                b0 = 4 * st
                self.mm_group(B[b0][:, 0:QT], [(wom[wb][:, hp, :], oT[0][hp][:, cs]) for hp in range(4)], reads=["wom%d" % wb], writes=["B%d" % b0])
                self.mm_group(B[b0 + 1][:, 0:QT], [(wof[wb][:, hp, :], oT[1][hp][:, cs]) for hp in range(4)], reads=["wof%d" % wb], writes=["B%d" % (b0 + 1)])
                self.mm_group(B[b0 + 2][:, 0:QT], [(wgm[wb][:, k, :], hTo[:, k, cs]) for k in range(8)], reads=["wgm%d" % wb, "hTo2"], writes=["B%d" % (b0 + 2)])
                self.mm_group(B[b0 + 3][:, 0:QT], [(wgf[wb][:, k, :], hTo[:, k, cs]) for k in range(8)], reads=["wgf%d" % wb, "hTo2"], writes=["B%d" % (b0 + 3)])
                P.add("act", lambda e, st=st, b0=b0: e.activation(out=sgm[st][:], in_=B[b0 + 2][:, 0:QT], func=AF.Sigmoid), reads=["B%d" % (b0 + 2)], writes=["sgm%d" % st])
                P.add("act", lambda e, st=st, b0=b0: e.activation(out=sgf[st][:], in_=B[b0 + 3][:, 0:QT], func=AF.Sigmoid), reads=["B%d" % (b0 + 3)], writes=["sgf%d" % st])
                P.add("dve", lambda e, st=st, b0=b0: e.tensor_tensor(out=t1[st][:], in0=sgm[st][:], in1=B[b0][:, 0:QT], op=ALU.mult), reads=["sgm%d" % st, "B%d" % b0], writes=["t1_%d" % st])
                P.add("dve", lambda e, st=st, b0=b0: e.tensor_tensor(out=t2[st][:], in0=sgf[st][:], in1=B[b0 + 1][:, 0:QT], op=ALU.mult), reads=["sgf%d" % st, "B%d" % (b0 + 1)], writes=["t2_%d" % st])
                P.add("dve", lambda e, st=st, dc=dc, cs=cs: e.tensor_tensor(out=yT[:, dc, cs], in0=t1[st][:], in1=t2[st][:], op=ALU.add), reads=["t1_%d" % st, "t2_%d" % st], writes=["yT%d_%d" % (dc, n)])

    def phase_merge2(self, xT, yT, x_own, w_out, gm):
        nc, P, A, B = self.nc, self.P, self.A, self.B
        ident_f = self.consts["ident_f"]
        wout = A.alloc("wout", [128, 8, D], BF16)
        self.dma("pool", wout[:], wview(w_out), writes=["wout"])
        xt = [A.alloc("xr%d" % i, [128, 4, D], F32) for i in range(2)]
        chunks = [(c * 512, 4, 128) for c in range(4)] + [(2048, 1, 32)]
        for ci, (t0, ntile, rows) in enumerate(chunks):
            xb = ci % 2
            if rows == 128:
                self.dma("sp", xt[xb][:], x_own[t0:t0 + 512, :].rearrange("(j p) d -> p j d", p=128), writes=["xr%d" % xb])
            else:
                self.dma("sp", xt[xb][0:32, 0, :], x_own[t0:t0 + 32, :], writes=["xr%d" % xb])
            ncol = ntile * rows
            for f in range(8):
                bk = f % 4

                def fn(e, f=f, bk=bk, xb=xb, ntile=ntile, rows=rows):
                    inst = None
                    for j in range(ntile):
                        inst = e.transpose(out=B[bk][:, j * rows:(j + 1) * rows], in_=xt[xb][0:rows, j, f * 128:(f + 1) * 128], identity=ident_f[0:rows, 0:rows])
                    return inst
                P.add("pe", fn, reads=["xr%d" % xb, "ident_f"], writes=["B%d" % bk])
                self.evac(self.evac_engine(), xT[:, f, t0:t0 + ncol], B[bk][:, 0:ncol], reads=["B%d" % bk], writes=["xT%d_%d" % (f, ci)])
        xk = ["xT%d_%d" % (f, ci) for f in range(8) for ci in range(5)]
        for dc in range(8):
            for n in range(NQT):
                cs = slice(n * QT, (n + 1) * QT)
                bk = 4 + (dc * 8 + n) % 4
                self.mm_group(B[bk][:, 0:QT], [(wout[:, k, dc * 128:(dc + 1) * 128], yT[:, k, cs]) for k in range(8)], reads=["wout"], writes=["B%d" % bk])
                P.add("dve", lambda e, bk=bk, dc=dc, cs=cs: e.scalar_tensor_tensor(out=xT[:, dc, cs], in0=B[bk][:, 0:QT], scalar=gm[:, dc:dc + 1], in1=xT[:, dc, cs], op0=ALU.mult, op1=ALU.add),
                      reads=["B%d" % bk] + xk, writes=["x1T%d_%d" % (dc, n)])

    def phase_ffn(self, xT, w_up, w_down, conv_w_c, conv_b_c, g_fin_r, gs2, sh2, gf, hval, out_d):
        nc, P, A, B = self.nc, self.P, self.A, self.B
        psS = self.psS
        ident_f, ones_f = self.consts["ident_f"], self.consts["ones_f"]
        h2T = A.alloc("h2T", [128, 8, TOWN], BF16)
        cw = A.alloc("cw", [128, 44, 3], F32)
        cb = A.alloc("cb", [128, 44], F32)
        gfin = A.alloc("gfin", [128, D], F32)
        self.dma("sp", cw[:], conv_w_c, writes=["cw"])
        self.dma("sp", cb[:], conv_b_c, writes=["cb"])
        self.dma("sp", gfin[:], g_fin_r[0:1, :].partition_broadcast(128), writes=["gfin"])
        f_mark = A.mark()
        sq = A.alloc("sq", [128, 8, QT], BF16)
        ones_b = A.alloc("ones_b", [128, 8], BF16)
        P.add("pool", lambda e: e.memset(ones_b[:], 1.0), writes=["ones_b"])
        tmpf = [A.alloc("tmpf%d" % i, [128, QT], F32) for i in range(2)]
        rowv = A.alloc("rowv", [1, TOWN], F32)
        for n in range(NQT):
            cs = slice(n * QT, (n + 1) * QT)
            P.add("dve", lambda e, cs=cs: e.tensor_tensor(out=sq[:], in0=xT[:, :, cs], in1=xT[:, :, cs], op=ALU.mult), reads=[], writes=["sq"])
            bk = 4 + n % 2
            self.mm_group(B[bk][0:1, 0:QT], [(ones_b[:, 0:1], sq[:, k, :]) for k in range(8)], reads=["sq", "ones_b"], writes=["B%d" % bk])
            P.add("dve", lambda e, bk=bk, cs=cs: e.tensor_scalar(out=rowv[0:1, cs], in0=B[bk][0:1, 0:QT], scalar1=1.0 / D, scalar2=EPS, op0=ALU.mult, op1=ALU.add),
                  reads=["B%d" % bk], writes=["rowv%d" % n])
            P.add("act", lambda e, cs=cs: e.activation(out=rowv[0:1, cs], in_=rowv[0:1, cs], func=AF.Sqrt), reads=["rowv%d" % n], writes=["rowv%d" % n])
            P.add("dve", lambda e, cs=cs: e.reciprocal(out=rowv[0:1, cs], in_=rowv[0:1, cs]), reads=["rowv%d" % n], writes=["rowv%d" % n])
        it = 0
        for n in range(NQT):
            cs = slice(n * QT, (n + 1) * QT)
            bk = 6 + n % 2
            P.add("pe", lambda e, bk=bk, cs=cs: e.matmul(B[bk][:, 0:QT], lhsT=ones_f[0:1, 0:128], rhs=rowv[0:1, cs], start=True, stop=True),
                  reads=["rowv%d" % n, "ones_f"], writes=["B%d" % bk])
            for k in range(8):
                tb = it % 2
                it += 1
                P.add("dve", lambda e, bk=bk, cs=cs, k=k, tb=tb: e.scalar_tensor_tensor(out=tmpf[tb][:], in0=xT[:, k, cs], scalar=gs2[:, k:k + 1], in1=B[bk][:, 0:QT], op0=ALU.mult, op1=ALU.mult),
                      reads=["B%d" % bk], writes=["tmpf%d" % tb])
                P.add("act", lambda e, cs=cs, k=k, tb=tb: e.activation(out=h2T[:, k, cs], in_=tmpf[tb][:], func=AF.Identity, bias=sh2[:, k:k + 1]),
                      reads=["tmpf%d" % tb], writes=["h2T%d_%d" % (k, n)])
        P.barrier()
        A.reset(f_mark)
        if self.stop_after == "F1":
            self.h2T_dbg = h2T
            return
        NSC = 2
        NPS = NQT // NSC
        aT = A.alloc("aT", [128, NFF, NPS * 256], BF16)
        wg = [A.alloc("wg%d" % i, [128, 8, 128], BF16) for i in range(2)]
        wv = [A.alloc("wv%d" % i, [128, 8, 128], BF16) for i in range(2)]
        wdn = [A.alloc("wdn%d" % i, [128, NFF, 128], BF16) for i in range(2)]
        cg = [A.alloc("cg%d" % i, [128, 2, 128], F32) for i in range(2)]
        cv = [A.alloc("cv%d" % i, [128, 2, 128], F32) for i in range(2)]
        sg = [A.alloc("sg%d" % i, [128, 2, 128], F32) for i in range(2)]
        ot = [A.alloc("ot%d" % i, [128, D], F32) for i in range(2)]
        ssqo = A.alloc("ssqo", [128, 2], F32)
        rvo = A.alloc("rvo", [128, 2], F32)
        rso = A.alloc("rso", [128, 2], F32)
        junko = A.alloc("junko", [128, D], BF16)
        it = 0
        wi = 0
        di = 0
        oi = 0
        for sc in range(NSC):
            for ffc in range(NFF):
                wb = wi % 2
                wi += 1
                self.dma("pool", wg[wb][:], wview(w_up[:, ffc * 128:(ffc + 1) * 128]), writes=["wg%d" % wb])
                self.dma("pool", wv[wb][:], wview(w_up[:, DFF + ffc * 128:DFF + (ffc + 1) * 128]), writes=["wv%d" % wb])
                for nl in range(NPS):
                    n = sc * NPS + nl
                    cs = slice(n * QT, (n + 1) * QT)
                    tb = it % 2
                    it += 1
                    bg = 4 + 2 * tb
                    bv = bg + 1
                    hk = ["h2T%d_%d" % (k, n) for k in range(8)]
                    self.mm_group(B[bg][:, 0:QT], [(wg[wb][:, k, :], h2T[:, k, cs]) for k in range(8)], reads=["wg%d" % wb], writes=["B%d" % bg])
                    self.mm_group(B[bv][:, 0:QT], [(wv[wb][:, k, :], h2T[:, k, cs]) for k in range(8)], reads=["wv%d" % wb], writes=["B%d" % bv])
                    if n == 0:
                        for bb in (bg, bv):
                            P.add("dve", lambda e, bb=bb: e.tensor_scalar(out=B[bb][:, 0:2], in0=B[bb][:, 0:2], scalar1=hval[:, 0:1], scalar2=None, op0=ALU.mult),
                                  reads=["B%d" % bb], writes=["B%d" % bb])
                    for (bb, cidx, dst, nm) in ((bg, ffc, cg, "cg"), (bv, NFF + ffc, cv, "cv")):
                        u3 = B[bb][:, 0:QT].rearrange("p (s i) -> p s i", i=SBW)
                        P.add("act", lambda e, u3=u3, cidx=cidx, dst=dst, tb=tb: e.activation(out=dst[tb][:], in_=u3[:, :, 2:130], func=AF.Identity, scale=cw[:, cidx, 2:3], bias=cb[:, cidx:cidx + 1]),
                              reads=["B%d" % bb, "cw", "cb"], writes=["%s%d" % (nm, tb)])
                        P.add("dve", lambda e, u3=u3, cidx=cidx, dst=dst, tb=tb: e.scalar_tensor_tensor(out=dst[tb][:], in0=u3[:, :, 1:129], scalar=cw[:, cidx, 1:2], in1=dst[tb][:], op0=ALU.mult, op1=ALU.add),
                              reads=["B%d" % bb, "%s%d" % (nm, tb)], writes=["%s%d" % (nm, tb)])
                        P.add("dve", lambda e, u3=u3, cidx=cidx, dst=dst, tb=tb: e.scalar_tensor_tensor(out=dst[tb][:], in0=u3[:, :, 0:128], scalar=cw[:, cidx, 0:1], in1=dst[tb][:], op0=ALU.mult, op1=ALU.add),
                              reads=["B%d" % bb, "%s%d" % (nm, tb)], writes=["%s%d" % (nm, tb)])
                    P.add("act", lambda e, tb=tb: e.activation(out=sg[tb][:], in_=cg[tb][:], func=AF.Silu), reads=["cg%d" % tb], writes=["sg%d" % tb])
                    P.add("dve", lambda e, tb=tb, ffc=ffc, nl=nl: e.tensor_tensor(out=aT[:, ffc, nl * 256:(nl + 1) * 256].rearrange("p (s i) -> p s i", i=128), in0=sg[tb][:], in1=cv[tb][:], op=ALU.mult),
                          reads=["sg%d" % tb, "cv%d" % tb], writes=["aT%d_%d" % (ffc, nl)])
            for dc in range(8):
                db = di % 2
                di += 1
                self.dma("pool", wdn[db][:], wview(w_down[:, dc * 128:(dc + 1) * 128]), writes=["wdn%d" % db])
                for nl in range(NPS):
                    n = sc * NPS + nl
                    cs = slice(n * QT, (n + 1) * QT)
                    bk = 4 + (dc * NPS + nl) % 4
                    self.mm_group(B[bk][:, 0:256], [(wdn[db][:, ffc, :], aT[:, ffc, nl * 256:(nl + 1) * 256]) for ffc in range(NFF)],
                                  reads=["wdn%d" % db] + ["aT%d_%d" % (ffc, nl) for ffc in range(NFF)], writes=["B%d" % bk])
                    xv = xT[:, dc, cs].rearrange("p (s i) -> p s i", i=SBW)[:, :, 2:130]
                    P.add("dve", lambda e, bk=bk, dc=dc, xv=xv: e.scalar_tensor_tensor(out=xv, in0=B[bk][:, 0:256].rearrange("p (s i) -> p s i", i=128), scalar=gf[:, dc:dc + 1], in1=xv, op0=ALU.mult, op1=ALU.add),
                          reads=["B%d" % bk], writes=["x2T%d_%d" % (dc, n)])
            for nl in range(NPS):
                n = sc * NPS + nl
                for s_ in range(2):
                    m = 2 * n + s_
                    ob = oi % 2
                    oi += 1
                    c0 = m * SBW + 2
                    flat = psS[ob][:].rearrange("p a b -> p (a b)")

                    def fn(e, ob=ob, c0=c0):
                        inst = None
                        for dc in range(8):
                            inst = e.transpose(out=psS[ob][:, dc // 4, (dc % 4) * 128:(dc % 4 + 1) * 128], in_=xT[:, dc, c0:c0 + 128], identity=ident_f[:])
                        return inst
                    P.add("pe", fn, reads=["x2T%d_%d" % (dc, n) for dc in range(8)] + ["ident_f"], writes=["S%d" % ob])
                    P.add("act", lambda e, flat=flat, ob=ob: e.activation(out=junko[:], in_=flat, func=AF.Square, accum_out=ssqo[:, ob:ob + 1]),
                          reads=["S%d" % ob], writes=["ssqo%d" % ob, "junko"])
                    P.add("dve", lambda e, ob=ob: e.tensor_scalar(out=rvo[:, ob:ob + 1], in0=ssqo[:, ob:ob + 1], scalar1=1.0 / D, scalar2=EPS, op0=ALU.mult, op1=ALU.add),
                          reads=["ssqo%d" % ob], writes=["rvo%d" % ob])
                    P.add("pool", lambda e, ob=ob: e.tensor_tensor(out=rso[:, ob:ob + 1], in0=rvo[:, ob:ob + 1], in1=self.mhalf[:, 0:1], op=ALU.pow),
                          reads=["rvo%d" % ob, "mhalf"], writes=["rso%d" % ob])
                    P.add("dve", lambda e, flat=flat, ob=ob: e.scalar_tensor_tensor(out=ot[ob][:], in0=flat, scalar=rso[:, ob:ob + 1], in1=gfin[:], op0=ALU.mult, op1=ALU.mult),
                          reads=["S%d" % ob, "rso%d" % ob, "gfin"], writes=["ot%d" % ob])
                    self.dma("sp", out_d[m * 128:(m + 1) * 128, :], ot[ob][:], reads=["ot%d" % ob], writes=["out%d" % m])

    def finish_dbg(self, dumps):
        nc, P = self.nc, self.P
        self.dbg_mark = self.A.mark()
        for name, (t, shape, dt) in dumps.items():
            d = nc.dram_tensor("dbg_" + name, list(shape), F32, kind="ExternalOutput").ap()
            if dt == F32:
                self.dma("sp", d, t[:], reads=[], writes=["dbg_" + name])
            else:
                tf = self.A.alloc("dbgf_" + name, list(shape), F32)
                P.add("dve", lambda e, tf=tf, t=t: e.tensor_copy(out=tf[:], in_=t[:]), writes=["dbgf_" + name])
                self.dma("sp", d, tf[:], reads=["dbgf_" + name], writes=["dbg_" + name])
        for name in self.dump_scr:
            ap = self.scr[name]
            shp = list(ap.shape)
            d = nc.dram_tensor("dbg_" + name, shp, F32, kind="ExternalOutput").ap()
            if len(shp) == 2:
                srcs = [(ap, d)]
            else:
                srcs = [(ap[i], d[i]) for i in range(shp[0])]
            for i, (sa, da) in enumerate(srcs):
                rows, cols = sa.shape
                tf = self.A.alloc("dbgs", [rows, cols], F32)
                self.dma("pool", tf[:], sa, writes=["dbgs%s%d" % (name, i)])
                self.dma("sp", da, tf[:], reads=["dbgs%s%d" % (name, i)], writes=["dbgo%s%d" % (name, i)])
                if (i % 4) == 3:
                    P.barrier()
                    self.A.reset(self.dbg_mark)
        self.dma("sp", self.out_d[0:128, 0:128], self.consts["ident_f"][:], writes=["outd"])
        jt = self.A.alloc("touch", [1, 64], F32)
        jti = self.A.alloc("touchi", [1, 64], I32)
        for i, (name, ap) in enumerate(self.din.items()):
            idx = tuple([slice(0, 1)] * len(ap.shape))
            src = ap[idx]
            while len(src.shape) > 2:
                src = src[0]
            tgt = jti if ap.dtype == I32 else jt
            self.dma("sp", tgt[0:1, i:i + 1], src, writes=["touch%d" % i])
        P.barrier()
        P.emit()
        return nc


def prep_inputs(inp):
    x = np.asarray(inp["x"], np.float32)
    cvec = np.asarray(inp["c"], np.float32)
    pos = np.asarray(inp["positions"], np.int32)
    w_in = np.asarray(inp["w_in"], np.float32)[0]
    o = [0, 384, 640, 672, 1184, 1696, 2208, 2216, 3240, 4264]
    w_cq, w_ckv_, w_krope, w_fq, w_fk, w_fv, w_fl, w_gm, w_gf = [np.ascontiguousarray(w_in[:, o[i]:o[i + 1]]) for i in range(9)]
    swap = np.concatenate([np.arange(16, 32), np.arange(0, 16)])
    w_kr = np.zeros((D, 96), np.float32)
    w_kr[:, 64:96] = w_krope
    w_kr[:, 0:8] = w_fl
    w_krs = np.zeros((D, 96), np.float32)
    w_krs[:, 64:96] = w_krope[:, swap]
    w_uq = np.asarray(inp["w_uq"], np.float32)[0]
    w_uqs = w_uq.copy().reshape(384, 8, 96)
    w_uqs[:, :, 64:96] = w_uqs[:, :, 64:96][:, :, swap]
    w_uqs = np.ascontiguousarray(w_uqs.reshape(384, 768))

    def col(v, n):
        return np.ascontiguousarray(np.asarray(v, np.float32).reshape(n, 128).T)
    invf = (10000.0 ** (-np.arange(0, 32, 2, dtype=np.float32) / np.float32(32))).astype(np.float32)
    invf_c = np.zeros((96, 1), np.float32)
    invf_c[64:80, 0] = invf
    invf_c[80:96, 0] = invf
    sgn_c = np.zeros((96, 1), np.float32)
    sgn_c[64:80] = -1.0
    sgn_c[80:96] = 1.0
    cmat = np.zeros((128, 128), np.float32)
    for cc in range(16):
        for h in range(8):
            for c2 in range(cc + 1, 16):
                cmat[cc * 8 + h, c2 * 8 + h] = 1.0
    conv_w = np.asarray(inp["conv_w"], np.float32)[0]
    conv_w_c = np.ascontiguousarray(conv_w.reshape(3, 44, 128).transpose(2, 1, 0))
    common = dict(
        w_ada=np.asarray(inp["w_ada"], np.float32)[0], b_ada_c=col(inp["b_ada"][0], 48),
        g_mix_c=col(inp["norm_mix_g"][0], 8), g_ffn_c=col(inp["norm_ffn_g"][0], 8),
        g_fin_r=np.asarray(inp["norm_final_g"], np.float32).reshape(1, D),
        g_q_c=col(inp["q_norm_g"][0], 3), g_kv_c=col(inp["kv_norm_g"][0], 2),
        b_fg_c=np.ascontiguousarray(np.tile(np.asarray(inp["b_forget"], np.float32)[0], 16).reshape(128, 1)),
        conv_w_c=conv_w_c, conv_b_c=col(inp["conv_b"][0], 44),
        w_ckv=w_ckv_, w_kr=w_kr, w_krs=w_krs, w_fk=w_fk, w_fv=w_fv, w_fl=w_fl, w_cq=w_cq, w_fq=w_fq, w_gm=w_gm, w_gf=w_gf,
        w_uq=w_uq, w_uqs=w_uqs, w_ukv=np.asarray(inp["w_ukv"], np.float32)[0],
        w_om=np.asarray(inp["w_o_mla"], np.float32)[0], w_of=np.asarray(inp["w_o_fox"], np.float32)[0],
        w_out=np.asarray(inp["w_out"], np.float32)[0], w_up=np.asarray(inp["w_up"], np.float32)[0],
        w_down=np.asarray(inp["w_down"], np.float32)[0],
        invf_c=invf_c, sgn_c=sgn_c, ident=np.eye(128, dtype=np.float32), cmat=cmat,
    )
    maps = []
    for b in range(2):
        for c in range(4):
            xo = np.zeros((NBLK, SBW, D), np.float32)
            po = np.zeros((NBLK, SBW), np.int32)
            for m in range(NBLK):
                r = 4 * m + c
                lo = 128 * r - 2
                if lo < 0:
                    xo[m, 2:] = x[b, 0:128]
                    po[m, 2:] = pos[b, 0:128]
                else:
                    xo[m] = x[b, lo:lo + SBW]
                    po[m] = pos[b, lo:lo + SBW]
            mk = np.zeros((128, 4, SBW), np.float32)
            sk = np.arange(128)[:, None]
            tq = np.arange(128)[None, :]
            for d in range(4):
                if d == c - 1:
                    mk[127, d, 0] = NEG
                elif d == c:
                    mk[:, d, 0:2] = NEG
                    mk[:, d, 2:] = np.where(sk > tq, NEG, 0.0)
                elif d > c:
                    mk[:, d, :] = NEG
            h1 = np.zeros((128, 2), np.float32)
            if c == 0:
                h1[127, 0] = NEG
            fl = np.zeros((128, 4), np.float32)
            fl[:, c] = 1.0
            hv = np.ones((128, 1), np.float32)
            m_ = dict(common)
            m_.update(x_kv=np.ascontiguousarray(x[b]), x_own=np.ascontiguousarray(xo.reshape(TOWN, D)),
                      pos_kv=np.ascontiguousarray(pos[b].reshape(1, S)), pos_own=np.ascontiguousarray(po.reshape(1, TOWN)),
                      c_col=col(cvec[b], 8), masks=mk, hm1=h1, flags=fl,
                      hvalid=(np.zeros((128, 1), np.float32) if c == 0 else hv))
            maps.append(m_)
    return maps


_NC_CACHE = {}


def kernel(**inputs):
    maps = prep_inputs(inputs)
    if "nc" not in _NC_CACHE:
        _NC_CACHE["nc"] = Builder().build()
    nc = _NC_CACHE["nc"]
    res = run_bass_kernel_spmd(nc, maps, core_ids=list(range(8)))
    out = np.zeros((2, S, D), np.float32)
    for b in range(2):
        for c in range(4):
            o = res.results[b * 4 + c]["out"].reshape(NBLK, 128, D)
            for m in range(NBLK):
                r = 4 * m + c
                out[b, 128 * r:128 * (r + 1)] = o[m]
    return out
```

```python
import os
import numpy as np
from contextlib import ExitStack
import concourse.bass as bass
import concourse.mybir as mybir
from concourse.bass_utils import run_bass_kernel_spmd

F32 = mybir.dt.float32
BF16 = mybir.dt.bfloat16
I32 = mybir.dt.int32
AF = mybir.ActivationFunctionType
ALU = mybir.AluOpType

D = 1024
S = 8192
NBLK = 16
SBW = 130
TOWN = NBLK * SBW
QT = 2 * SBW
NQT = 8
DFF = 2816
NFF = 22
EPS = 1e-6
NEG = -30000.0
MLA_SCALE = float(1.0 / np.sqrt(96.0))
SB_BASE = 16512
SB_LIMIT = 229248


class Op:
    __slots__ = ("eng", "fn", "deps", "needs_inc", "semval", "is_dma", "dsem", "dval")

    def __init__(self, eng, fn, is_dma=False):
        self.eng = eng
        self.fn = fn
        self.deps = []
        self.needs_inc = False
        self.semval = 0
        self.is_dma = is_dma
        self.dsem = None
        self.dval = 0


class Prog:
    ENGS = ("pe", "act", "dve", "pool", "sp")
    NDS = 8

    def __init__(self, nc):
        self.nc = nc
        self.ops = {e: [] for e in self.ENGS}
        self.last_write = {}
        self.readers = {}
        self.ndma = {e: 0 for e in self.ENGS}
        self.dma_ops = {e: [] for e in self.ENGS}

    def add(self, eng, fn, reads=(), writes=(), dma=False):
        op = Op(eng, fn, dma)
        deps = []
        for r in reads:
            lw = self.last_write.get(r)
            if lw is not None:
                deps.append(lw)
        for w in writes:
            lw = self.last_write.get(w)
            if lw is not None:
                deps.append(lw)
            deps.extend(self.readers.get(w, ()))
        seen = set()
        for d in deps:
            if id(d) not in seen:
                seen.add(id(d))
                op.deps.append(d)
        for r in reads:
            self.readers.setdefault(r, []).append(op)
        for w in writes:
            self.last_write[w] = op
            self.readers[w] = []
        if dma:
            i = self.ndma[eng]
            self.ndma[eng] += 1
            op.dsem = i % self.NDS
            op.dval = 16 * (i // self.NDS + 1)
            if i >= self.NDS:
                op.deps.append(self.dma_ops[eng][i - self.NDS])
            self.dma_ops[eng].append(op)
        self.ops[eng].append(op)
        return op

    def barrier(self):
        lasts = []
        for e in self.ENGS:
            real = [o for o in self.ops[e][-64:] if o.fn is not None and not o.is_dma]
            if real:
                lasts.append(real[-1])
            lasts.extend(self.dma_ops[e][-self.NDS:])
        for e in self.ENGS:
            op = Op(e, None)
            op.deps = list(lasts)
            self.ops[e].append(op)
        self.last_write = {}
        self.readers = {}

    def emit(self):
        nc = self.nc
        for e in self.ENGS:
            for op in self.ops[e]:
                for d in op.deps:
                    if d.is_dma:
                        continue
                    if d.eng == e and e == "pe":
                        continue
                    d.needs_inc = True
        for e in self.ENGS:
            c = 0
            for op in self.ops[e]:
                if op.needs_inc and not op.is_dma:
                    c += 1
                    op.semval = c
        with ExitStack() as st:
            sems = {e: st.enter_context(nc.semaphore("s_" + e)) for e in self.ENGS}
            dsems = {e: [st.enter_context(nc.semaphore("d_%s%d" % (e, i))) for i in range(self.NDS)]
                     for e in ("sp", "pool")}
            block = st.enter_context(nc.Block())
            engobj = {"pe": block.tensor, "act": block.scalar, "dve": block.vector,
                      "pool": block.gpsimd, "sp": block.sync}

            def body(e):
                def run(eng):
                    waited = {}
                    for op in self.ops[e]:
                        for d in op.deps:
                            if d.is_dma:
                                key = ("d", d.eng, d.dsem)
                                sem = dsems[d.eng][d.dsem]
                                val = d.dval
                            else:
                                if d.eng == e and e == "pe":
                                    continue
                                key = ("e", d.eng)
                                sem = sems[d.eng]
                                val = d.semval
                            if waited.get(key, 0) >= val:
                                continue
                            waited[key] = val
                            eng.wait_ge(sem, val)
                        if op.fn is None:
                            continue
                        inst = op.fn(eng)
                        if op.is_dma:
                            inst.then_inc(dsems[e][op.dsem], 16)
                        elif op.needs_inc:
                            inst.then_inc(sems[e], 1)
                return run

            for e in self.ENGS:
                engobj[e](body(e))


class Arena:
    def __init__(self, nc):
        self.nc = nc
        self.off = SB_BASE
        self.top = SB_LIMIT
        self.n = 0

    def alloc(self, name, shape, dt):
        sz = int(np.prod(shape[1:])) * mybir.dt.size(dt)
        sz = (sz + 63) // 64 * 64
        assert self.off + sz <= self.top, ("SBUF overflow", name, self.off, sz, self.top)
        self.n += 1
        t = self.nc.alloc_sbuf_tensor_at("%s_%d" % (name, self.n), list(shape), dt, offset=self.off)
        self.off += sz
        return t

    def alloc_top(self, name, shape, dt):
        sz = int(np.prod(shape[1:])) * mybir.dt.size(dt)
        sz = (sz + 63) // 64 * 64
        self.top -= sz
        assert self.top >= self.off, ("SBUF overflow(top)", name, self.off, self.top)
        self.n += 1
        return self.nc.alloc_sbuf_tensor_at("%s_%d" % (name, self.n), list(shape), dt, offset=self.top)

    def mark(self):
        return self.off

    def reset(self, m):
        self.off = m


def wview(ap, p=128):
    return ap.rearrange("(k p) c -> p k c", p=p)


class Builder:
    def __init__(self, stop_after=None, dbg=False):
        self.stop_after = stop_after
        self.dbg = dbg
        self.dump_scr = []
        self.nc = nc = bass.Bass("TRN2", target_bir_lowering=False)
        self.P = Prog(nc)
        self.A = Arena(nc)
        self.din = {}
        self.scr = {}
        self.rot = 0
        self.alt = 0

    def inp(self, name, shape, dt=F32):
        self.din[name] = self.nc.dram_tensor(name, list(shape), dt, kind="ExternalInput").ap()
        return self.din[name]

    def scratch(self, name, shape, dt):
        ap = self.nc.dram_tensor(name, list(shape), dt, kind="Internal").ap()
        self.scr[name] = ap
        return ap

    def dma(self, q, out, in_, reads=(), writes=(), **kw):
        return self.P.add(q, lambda e: e.dma_start(out=out, in_=in_, **kw), reads=reads, writes=writes, dma=True)

    def evac_engine(self):
        self.alt ^= 1
        return "act" if self.alt else "dve"

    def evac(self, eng, out, in_, scale=None, bias=None, reads=(), writes=()):
        if eng == "act":
            def fn(e):
                kw = {}
                if scale is not None:
                    kw["scale"] = scale
                if bias is not None:
                    kw["bias"] = bias
                func = AF.Identity if (bias is not None and not isinstance(bias, float)) or (scale is not None and not isinstance(scale, float)) else AF.Copy
                return e.activation(out=out, in_=in_, func=func, **kw)
        else:
            def fn(e):
                if scale is None and bias is None:
                    return e.tensor_copy(out=out, in_=in_)
                if bias is None:
                    return e.tensor_scalar(out=out, in0=in_, scalar1=scale, scalar2=None, op0=ALU.mult)
                s = 1.0 if scale is None else scale
                return e.tensor_scalar(out=out, in0=in_, scalar1=s, scalar2=bias, op0=ALU.mult, op1=ALU.add)
        return self.P.add(eng, fn, reads=reads, writes=writes)

    def bank_rot(self, lo=4, n=4):
        b = lo + self.rot % n
        self.rot += 1
        return b

    def mm_group(self, out, pairs, reads=(), writes=()):
        def fn(e):
            inst = None
            n = len(pairs)
            for i, (l, r) in enumerate(pairs):
                inst = e.matmul(out, lhsT=l, rhs=r, start=(i == 0), stop=(i == n - 1))
            return inst
        return self.P.add("pe", fn, reads=reads, writes=writes)

    def build(self):
        nc, P, A = self.nc, self.P, self.A
        inp = self.inp
        x_kv = inp("x_kv", [S, D])
        x_own = inp("x_own", [TOWN, D])
        pos_kv = inp("pos_kv", [1, S], I32)
        pos_own = inp("pos_own", [1, TOWN], I32)
        c_col = inp("c_col", [128, 8])
        w_ada = inp("w_ada", [D, 6 * D])
        b_ada_c = inp("b_ada_c", [128, 48])
        g_mix_c = inp("g_mix_c", [128, 8])
        g_ffn_c = inp("g_ffn_c", [128, 8])
        g_fin_r = inp("g_fin_r", [1, D])
        g_q_c = inp("g_q_c", [128, 3])
        g_kv_c = inp("g_kv_c", [128, 2])
        b_fg_c = inp("b_fg_c", [128, 1])
        conv_w_c = inp("conv_w_c", [128, 44, 3])
        conv_b_c = inp("conv_b_c", [128, 44])
        w_ckv = inp("w_ckv", [D, 256])
        w_kr = inp("w_kr", [D, 96])
        w_krs = inp("w_krs", [D, 96])
        w_fk = inp("w_fk", [D, 512])
        w_fv = inp("w_fv", [D, 512])
        w_fl = inp("w_fl", [D, 8])
        w_cq = inp("w_cq", [D, 384])
        w_fq = inp("w_fq", [D, 512])
        w_gm = inp("w_gm", [D, D])
        w_gf = inp("w_gf", [D, D])
        w_uq = inp("w_uq", [384, 768])
        w_uqs = inp("w_uqs", [384, 768])
        w_ukv = inp("w_ukv", [256, 1024])
        w_om = inp("w_om", [512, D])
        w_of = inp("w_of", [512, D])
        w_out = inp("w_out", [D, D])
        w_up = inp("w_up", [D, 2 * DFF])
        w_down = inp("w_down", [DFF, D])
        masks = inp("masks", [128, 4, SBW])
        hm1 = inp("hm1", [128, 2])
        flags = inp("flags", [128, 4])
        hvalid = inp("hvalid", [128, 1])
        invf_c = inp("invf_c", [96, 1])
        sgn_c = inp("sgn_c", [96, 1])
        ident_d = inp("ident", [128, 128])
        cmat_d = inp("cmat", [128, 128])
        out_d = nc.dram_tensor("out", [NBLK * 128, D], F32, kind="ExternalOutput").ap()
        self.out_d = out_d

        KMd = self.scratch("KMd", [8, 64, S], BF16)
        KRd = self.scratch("KRd", [32, S], BF16)
        VMd = self.scratch("VMd", [8, 64, S], BF16)
        KFd = self.scratch("KFd", [8, 67, S], BF16)
        VFd = self.scratch("VFd", [8, 64, S], BF16)
        FQd = self.scratch("FQd", [3, 128, SBW], BF16)
        HTd = self.scratch("HTd", [128, 8, TOWN], BF16)
        QFd = self.scratch("QFd", [8, 64, TOWN], BF16)
        X1d = self.scratch("X1d", [128, 8, TOWN], F32)

        psS = [nc.alloc_psum_tensor("psS%d" % i, [128, 2, 512], F32) for i in range(3)]
        psB = [nc.alloc_psum_tensor("psB%d" % i, [128, 512], F32) for i in range(2)]
        B = [psS[i][:, j, :] for i in range(3) for j in range(2)] + [t[:] for t in psB]
        self.B = B
        self.psS = psS
        self.psB_raw = psB

        ident_f = A.alloc("ident_f", [128, 128], F32)
        ident_b = A.alloc("ident_b", [128, 128], BF16)
        ones_f = A.alloc("ones_f", [128, 512], F32)
        mhalf = A.alloc("mhalf", [128, 8], F32)
        ada = A.alloc("ada", [128, 48], F32)
        gs1 = A.alloc("gs1", [128, 8], F32)
        gs2 = A.alloc("gs2", [128, 8], F32)
        gq = A.alloc("gq", [128, 3], F32)
        gkv = A.alloc("gkv", [128, 2], F32)
        invf = A.alloc("invf", [96, 1], F32)
        sgn = A.alloc("sgn", [96, 1], F32)
        flg = A.alloc("flg", [128, 4], F32)
        hval = A.alloc("hval", [128, 1], F32)
        bada = A.alloc("bada", [128, 48], F32)
        gffn = A.alloc("gffn", [128, 8], F32)
        cact = A.alloc("cact", [128, 8], F32)
        maskb = A.alloc("maskb", [128, 4, SBW], BF16)
        hm1b = A.alloc("hm1b", [128, 2], BF16)
        self.consts = dict(ident_f=ident_f, ident_b=ident_b, ones_f=ones_f)

        self.dma("sp", ident_f[:], ident_d, writes=["ident_f"])
        self.dma("pool", ident_b[:], ident_d, writes=["ident_b"])
        self.dma("pool", maskb[:], masks, writes=["maskb"])
        self.dma("pool", hm1b[:], hm1, writes=["hm1b"])
        self.dma("sp", gq[:], g_q_c, writes=["gq"])
        self.dma("sp", gkv[:], g_kv_c, writes=["gkv"])
        self.dma("sp", invf[:], invf_c, writes=["invf"])
        self.dma("sp", sgn[:], sgn_c, writes=["sgn"])
        self.dma("sp", flg[:], flags, writes=["flg"])
        self.dma("sp", hval[:], hvalid, writes=["hval"])
        P.add("pool", lambda e: e.memset(ones_f[:], 1.0), writes=["ones_f"])
        P.add("pool", lambda e: e.memset(mhalf[:], -0.5), writes=["mhalf"])
        self.mhalf = mhalf

        persist_mark = A.mark()
        if self.stop_after == "C":
            return self.finish_dbg({"maskb": (maskb, [128, 4, SBW], BF16)})

        c_sb = A.alloc("c_sb", [128, 8], F32)
        gmix = A.alloc("gmix", [128, 8], F32)
        tmp8 = A.alloc("tmp8", [128, 8], F32)
        wab = [A.alloc("wab%d" % i, [128, 8, D], F32) for i in range(2)]
        self.dma("sp", c_sb[:], c_col, writes=["c_sb"])
        self.dma("sp", bada[:], b_ada_c, writes=["bada"])
        self.dma("sp", gmix[:], g_mix_c, writes=["gmix"])
        self.dma("sp", gffn[:], g_ffn_c, writes=["gffn"])
        P.add("act", lambda e: e.activation(out=cact[:], in_=c_sb[:], func=AF.Silu), reads=["c_sb"], writes=["cact"])
        for v in range(2):
            wb = wab[v % 2]
            self.dma("sp" if v % 2 == 0 else "pool", wb[:], wview(w_ada[:, v * D:(v + 1) * D]), writes=["wab%d" % (v % 2)])

            def fn(e, v=v, wb=wb):
                inst = None
                for j in range(8):
                    for k in range(8):
                        inst = e.matmul(B[4][:, v * 8 + j:v * 8 + j + 1], lhsT=wb[:, k, j * 128:(j + 1) * 128],
                                        rhs=cact[:, k:k + 1], start=(k == 0), stop=(k == 7))
                return inst
            P.add("pe", fn, reads=["wab%d" % (v % 2), "cact"], writes=["B4"])
        P.add("dve", lambda e: e.tensor_tensor(out=ada[:, 0:16], in0=B[4][:, 0:16], in1=bada[:, 0:16], op=ALU.add),
              reads=["B4", "bada"], writes=["ada"])
        P.add("dve", lambda e: e.tensor_scalar_add(out=tmp8[:], in0=ada[:, 8:16], scalar1=1.0), reads=["ada"], writes=["tmp8"])
        P.add("dve", lambda e: e.tensor_tensor(out=gs1[:], in0=tmp8[:], in1=gmix[:], op=ALU.mult), reads=["tmp8", "gmix"], writes=["gs1"])
        sh1 = ada[:, 0:8]
        sh2 = ada[:, 24:32]
        gm = ada[:, 16:24]
        gf = ada[:, 40:48]
        P.barrier()
        A.reset(persist_mark)
        if self.stop_after == "A":
            return self.finish_dbg({"ada": (ada, [128, 48], F32)})

        self.qw = dict(wcq=A.alloc_top("wcq", [128, 8, 384], BF16), wfq=A.alloc_top("wfq", [128, 8, 512], BF16),
                       wuq=A.alloc_top("wuq", [128, 3, 768], BF16), wuqs=A.alloc_top("wuqs", [128, 3, 768], BF16))
        self.ada_ctx = dict(w_ada=w_ada, ada=ada, bada=bada, cact=cact, gffn=gffn, gs2=gs2)
        self.phase_kv(x_kv, pos_kv, w_ckv, w_kr, w_krs, w_fk, w_fv, w_fl, w_ukv, b_fg_c, cmat_d,
                      gs1, sh1, gkv, invf, sgn, flg, KMd, KRd, VMd, KFd, VFd, FQd)
        for t, d, nm in ((self.qw["wcq"], w_cq, "wcq"), (self.qw["wfq"], w_fq, "wfq"), (self.qw["wuq"], w_uq, "wuq"), (self.qw["wuqs"], w_uqs, "wuqs")):
            pass
        P.barrier()
        A.reset(persist_mark)
        if self.stop_after == "KV":
            self.dump_scr = ["KMd", "KRd", "VMd", "KFd", "VFd", "FQd"]
            return self.finish_dbg({})

        Qreg = [A.alloc("Q%d" % h, [128, TOWN], BF16) for h in range(8)]
        q_mark = A.mark()
        self.phase_q(x_own, pos_own, w_cq, w_fq, w_uq, w_uqs, gs1, sh1, gq, invf, sgn, Qreg, HTd, QFd)
        P.barrier()
        A.reset(q_mark)
        A.top = SB_LIMIT
        if self.stop_after == "Q":
            self.dump_scr = ["QFd", "HTd"]
            return self.finish_dbg({"Q%d" % h: (Qreg[h], [96, TOWN], BF16) for h in (0, 5)})

        oT = [[A.alloc("oT%d_%d" % (br, hp), [128, TOWN], BF16) for hp in range(4)] for br in range(2)]
        att_mark = A.mark()
        self.phase_att(Qreg, oT, maskb, hm1b, KMd, KRd, VMd, KFd, VFd, FQd, QFd)
        P.barrier()
        A.reset(att_mark)
        if self.stop_after == "ATT":
            return self.finish_dbg({"oT%d_%d" % (br, hp): (oT[br][hp], [128, TOWN], BF16) for br in range(2) for hp in (0, 3)})

        yT = A.alloc_top("yT", [128, 8, TOWN], BF16)
        self.phase_merge1(oT, yT, HTd, w_om, w_of, w_gm, w_gf)
        P.barrier()
        A.reset(persist_mark)
        if self.stop_after == "M1":
            return self.finish_dbg({"yT": (yT, [128, 8, TOWN], BF16)})
        xT = A.alloc("xT", [128, 8, TOWN], F32)
        x_mark = A.mark()
        self.phase_merge2(xT, yT, x_own, w_out, gm)
        P.barrier()
        A.reset(x_mark)
        A.top = SB_LIMIT
        if self.stop_after == "M2":
            return self.finish_dbg({"xT": (xT, [128, 8, TOWN], F32)})
        self.phase_ffn(xT, w_up, w_down, conv_w_c, conv_b_c, g_fin_r, gs2, sh2, gf, hval, out_d)
        if self.stop_after == "F1":
            return self.finish_dbg({"h2T": (self.h2T_dbg, [128, 8, TOWN], BF16), "xT": (xT, [128, 8, TOWN], F32)})
        P.barrier()
        P.emit()
        return nc

    def rope_tables(self, eng, posi, ncol, cc, ss, ti, tf, a, key, sink=None, okey=None):
        P = self.P
        add = sink or P.add
        okey = okey or key
        invf, sgn = self.invf_t, self.sgn_t
        r = slice(64, 96)
        inv2pi = float(np.float32(1.0 / (2 * np.pi)))
        c1 = float(np.float32(6.28125))
        c2 = float(np.float32(2 * np.pi - 6.28125))
        c3 = float(2 * np.pi - np.float64(np.float32(6.28125)) - np.float64(np.float32(2 * np.pi - 6.28125)))
        pi = float(np.pi)
        k = key
        add(eng, lambda e: e.tensor_copy(out=tf, in_=posi), reads=[okey + "posi"], writes=[k + "tf"])
        add(eng, lambda e: e.tensor_scalar(out=a, in0=tf, scalar1=invf[r, 0:1], scalar2=None, op0=ALU.mult),
              reads=[k + "tf", "invf"], writes=[k + "ang"])
        add(eng, lambda e: e.tensor_scalar(out=ti, in0=a, scalar1=inv2pi, scalar2=None, op0=ALU.mult),
              reads=[k + "ang"], writes=[k + "ti"])
        add(eng, lambda e: e.tensor_copy(out=tf, in_=ti), reads=[k + "ti"], writes=[k + "tf"])
        for cc_ in (c1, c2, c3):
            add(eng, lambda e, cc_=cc_: e.scalar_tensor_tensor(out=a, in0=tf, scalar=-cc_, in1=a, op0=ALU.mult, op1=ALU.add),
                  reads=[k + "tf", k + "ang"], writes=[k + "ang"])
        add(eng, lambda e: e.tensor_scalar(out=tf, in0=a, scalar1=pi, scalar2=-2 * pi, op0=ALU.is_gt, op1=ALU.mult),
              reads=[k + "ang"], writes=[k + "tf"])
        add(eng, lambda e: e.tensor_tensor(out=a, in0=a, in1=tf, op=ALU.add), reads=[k + "ang", k + "tf"], writes=[k + "ang"])
        add(eng, lambda e: e.tensor_scalar(out=tf, in0=a, scalar1=-pi, scalar2=2 * pi, op0=ALU.is_lt, op1=ALU.mult),
              reads=[k + "ang"], writes=[k + "tf"])
        add(eng, lambda e: e.tensor_tensor(out=a, in0=a, in1=tf, op=ALU.add), reads=[k + "ang", k + "tf"], writes=[k + "ang"])
        add("act", lambda e: e.activation(out=ss, in_=a, func=AF.Sin, scale=sgn[r, 0:1]),
              reads=[k + "ang", "sgn"], writes=[okey + "ss"])
        add(eng, lambda e: e.tensor_scalar_add(out=a, in0=a, scalar1=pi / 2), reads=[k + "ang", k + "ss"], writes=[k + "ang"])
        add(eng, lambda e: e.tensor_scalar(out=tf, in0=a, scalar1=pi, scalar2=-2 * pi, op0=ALU.is_gt, op1=ALU.mult),
              reads=[k + "ang"], writes=[k + "tf"])
        add(eng, lambda e: e.tensor_tensor(out=a, in0=a, in1=tf, op=ALU.add), reads=[k + "ang", k + "tf"], writes=[k + "ang"])
        add("act", lambda e: e.activation(out=cc, in_=a, func=AF.Sin), reads=[k + "ang"], writes=[okey + "cc"])

    def nt_stats(self, tiles, nf, key, scr, sink=None, scale_sink=None):
        P = self.P
        sink = sink or P.add
        scale_sink = scale_sink or sink
        nt = len(tiles)
        ssq, rv, rstd, junk = scr["ssq"], scr["rv"], scr["rstd"], scr["junk"]
        F = nf * 128
        for j, (src, rows, skeys, is_psum, tmp) in enumerate(tiles):
            sink("act", lambda e, j=j, src=src, rows=rows: e.activation(out=junk[0:rows, 0:F], in_=src, func=AF.Square, accum_out=ssq[0:rows, j:j + 1]),
                 reads=list(skeys), writes=[key + "ssq%d" % j, key + "junk"])
        rmax = max(t[1] for t in tiles)
        sink("dve", lambda e: e.tensor_scalar(out=rv[0:rmax, 0:nt], in0=ssq[0:rmax, 0:nt], scalar1=1.0 / F, scalar2=EPS, op0=ALU.mult, op1=ALU.add),
             reads=[key + "ssq%d" % j for j in range(nt)], writes=[key + "rv"])
        sink("pool", lambda e: e.tensor_tensor(out=rstd[0:rmax, 0:nt], in0=rv[0:rmax, 0:nt], in1=self.mhalf[0:rmax, 0:nt], op=ALU.pow),
             reads=[key + "rv", "mhalf"], writes=[key + "rstd"])
        srcs = []
        for j, (src, rows, skeys, is_psum, tmp) in enumerate(tiles):
            dst = tmp if is_psum else src
            wk = [key + "xn%d" % j] if is_psum else list(skeys)
            scale_sink("dve", lambda e, j=j, src=src, dst=dst, rows=rows: e.tensor_scalar(out=dst, in0=src, scalar1=rstd[0:rows, j:j + 1], scalar2=None, op0=ALU.mult),
                       reads=list(skeys) + [key + "rstd"], writes=wk)
            srcs.append((dst, rows, wk))
        return srcs

    def nt_trans(self, srcs, nf, dst_fn, gain, bias):
        P, B = self.P, self.B
        ident_f = self.consts["ident_f"]
        for f in range(nf):
            bk = f % 2

            def fn(e, f=f, bk=bk):
                inst = None
                col = 0
                for (dst, rows, wk) in srcs:
                    inst = e.transpose(out=B[bk][:, col:col + rows], in_=dst[:, f * 128:(f + 1) * 128], identity=ident_f[0:rows, 0:rows])
                    col += rows
                return inst
            rk = []
            for (_, _, wk) in srcs:
                rk += wk
            P.add("pe", fn, reads=rk + ["ident_f"], writes=["B%d" % bk])
            ncol = sum(s_[1] for s_ in srcs)
            out_ap, okeys = dst_fn(f, ncol)
            self.evac(self.evac_engine(), out_ap, B[bk][:, 0:ncol], scale=gain[:, f:f + 1],
                      bias=(bias[:, f:f + 1] if bias is not None else None), reads=["B%d" % bk], writes=okeys)

    def norm_transpose(self, tiles, nf, dst_fn, gain, bias, key, scr):
        srcs = self.nt_stats(tiles, nf, key, scr)
        self.nt_trans(srcs, nf, dst_fn, gain, bias)

    def phase_kv(self, x_kv, pos_kv, w_ckv, w_kr, w_krs, w_fk, w_fv, w_fl, w_ukv, b_fg_c, cmat_d,
                 gs1, sh1, gkv, invf, sgn, flg, KMd, KRd, VMd, KFd, VFd, FQd):
        nc, P, A, B = self.nc, self.P, self.A, self.B
        self.invf_t, self.sgn_t = invf, sgn
        wckv = A.alloc("wckv", [128, 8, 256], BF16)
        wkr = A.alloc("wkr", [128, 8, 96], BF16)
        wkrs = A.alloc("wkrs", [128, 8, 96], BF16)
        wfk = A.alloc("wfk", [128, 8, 512], BF16)
        wfv = A.alloc("wfv", [128, 8, 512], BF16)
        wfl = A.alloc("wfl", [128, 8, 8], BF16)
        wukv = A.alloc("wukv", [128, 2, 1024], BF16)
        for t, d, nm in ((wckv, w_ckv, "wckv"), (wkr, w_kr, "wkr"), (wkrs, w_krs, "wkrs"), (wfk, w_fk, "wfk"),
                         (wfv, w_fv, "wfv"), (wfl, w_fl, "wfl"), (wukv, w_ukv, "wukv")):
            self.dma("pool", t[:], wview(d), writes=[nm])
        xt = [A.alloc("xt%d" % i, [128, 4, D], F32) for i in range(2)]
        hT = [A.alloc("hT%d" % i, [128, 8, 512], BF16) for i in range(2)]
        junk = A.alloc("junk", [128, D], BF16)
        scr = dict(ssq=A.alloc("ssq", [128, 4], F32), rv=A.alloc("rv", [128, 4], F32), rstd=A.alloc("rstd", [128, 4], F32), junk=junk)
        junk2 = A.alloc("junk2", [128, 256], BF16)
        scr2 = dict(ssq=A.alloc("ssq2", [128, 4], F32), rv=A.alloc("rv2", [128, 4], F32), rstd=A.alloc("rstd2", [128, 4], F32), junk=junk2)
        latn = A.alloc("latn", [128, 4, 256], F32)
        ckvT = A.alloc("ckvT", [128, 2, 512], BF16)
        stgM = [A.alloc("stgM%d" % i, [128, 8, 512], BF16) for i in range(2)]
        stgK = [A.alloc("stgK%d" % i, [128, 4, 512], BF16) for i in range(2)]
        stgV = [A.alloc("stgV%d" % i, [128, 4, 512], BF16) for i in range(2)]
        krst = [A.alloc("krst%d" % i, [96, 512], BF16) for i in range(2)]
        posi = A.alloc("posi", [96, 512], I32)
        tmp_i = A.alloc("tmp_i", [96, 512], I32)
        tmp_f = A.alloc("tmp_f", [96, 512], F32)
        ang = A.alloc("ang", [96, 512], F32)
        ccK = A.alloc("ccK", [96, 512], F32)
        ssK = A.alloc("ssK", [96, 512], F32)
        rt1 = A.alloc("rt1", [96, 512], F32)
        rt2 = A.alloc("rt2", [96, 512], F32)
        LFraw = A.alloc("LFraw", [128, 512], F32)
        lfst = [A.alloc("lfst%d" % i, [8, 512], F32) for i in range(2)]

        for nm, d in (("wcq", self.din["w_cq"]), ("wfq", self.din["w_fq"]), ("wuq", self.din["w_uq"]), ("wuqs", self.din["w_uqs"])):
            self.dma("pool", self.qw[nm][:], wview(d), writes=["q_" + nm])
        ac = self.ada_ctx
        wah = A.alloc("wah", [128, 8, 512], F32)
        posi2 = [posi, A.alloc("posi_b", [96, 512], I32)]
        ccK2 = [ccK, A.alloc("ccK_b", [96, 512], F32)]
        ssK2 = [ssK, A.alloc("ssK_b", [96, 512], F32)]
        rr = slice(64, 96)

        class Lazy:
            def __init__(self):
                self.q = []

            def add(self, eng, fn, reads=(), writes=()):
                self.q.append((eng, fn, list(reads), list(writes)))

            def emit(self, n):
                for _ in range(min(n, len(self.q))):
                    eng, fn, r, w = self.q.pop(0)
                    P.add(eng, fn, reads=r, writes=w)

            def flush(self):
                self.emit(len(self.q))

        def prep(c, lazy):
            xb = c % 2
            tok = slice(c * 512, (c + 1) * 512)
            self.dma("sp", xt[xb][:], x_kv[tok, :].rearrange("(j p) d -> p j d", p=128), writes=["xt%d" % xb])
            self.dma("sp", posi2[xb][64:96, :], pos_kv[0:1, tok].partition_broadcast(32), writes=["K%dposi" % xb])
            tiles = [(xt[xb][:, j, :], 128, ["xt%d" % xb], False, None) for j in range(4)]
            tail = Lazy()
            srcs = self.nt_stats(tiles, 8, "kx", scr, sink=lazy.add, scale_sink=tail.add)
            self.rope_tables("dve", posi2[xb][rr, :], 512, ccK2[xb][rr, :], ssK2[xb][rr, :], tmp_i[rr, :], tmp_f[rr, :], ang[rr, :], "K",
                             sink=lazy.add, okey="K%d" % xb)
            lazy.q.extend(tail.q)
            return srcs

        def trans(c, srcs):
            hb = c % 2

            def dst_fn(f, ncol, hb=hb):
                return hT[hb][:, f, 0:ncol], ["hT%d_%d" % (hb, f)]
            self.nt_trans(srcs, 8, dst_fn, gs1, sh1)

        lz = Lazy()
        srcs_next = prep(0, lz)
        lz.flush()
        trans(0, srcs_next)
        for c in range(16):
            xb = c % 2
            hb = c % 2
            sb = c % 2
            tok = slice(c * 512, (c + 1) * 512)
            hkeys = ["hT%d_%d" % (hb, f) for f in range(8)]
            for j in range(4):
                bk = 2 + j // 2
                self.mm_group(B[bk][:, (j % 2) * 256:(j % 2) * 256 + 256],
                              [(hT[hb][:, k, j * 128:(j + 1) * 128], wckv[:, k, :]) for k in range(8)],
                              reads=hkeys + ["wckv"], writes=["B%d" % bk])
            ltiles = [(B[2 + j // 2][:, (j % 2) * 256:(j % 2) * 256 + 256], 128, ["B%d" % (2 + j // 2)], True, latn[:, j, :]) for j in range(4)]
            lsrcs = self.nt_stats(ltiles, 2, "kl", scr2)
            lz = Lazy()
            if c + 1 < 16:
                srcs_next = prep(c + 1, lz)
            bk1 = self.bank_rot()
            self.mm_group(B[bk1][0:96, 0:512], [(wkr[:, k, :], hT[hb][:, k, :]) for k in range(8)], reads=hkeys + ["wkr"], writes=["B%d" % bk1])
            bk2 = self.bank_rot()
            self.mm_group(B[bk2][0:96, 0:512], [(wkrs[:, k, :], hT[hb][:, k, :]) for k in range(8)], reads=hkeys + ["wkrs"], writes=["B%d" % bk2])
            P.add("dve", lambda e, bk1=bk1, xb=xb: e.tensor_tensor(out=rt1[rr, :], in0=B[bk1][rr, 0:512], in1=ccK2[xb][rr, :], op=ALU.mult),
                  reads=["B%d" % bk1, "K%dcc" % xb], writes=["rt1"])
            P.add("dve", lambda e, bk2=bk2, xb=xb: e.tensor_tensor(out=rt2[rr, :], in0=B[bk2][rr, 0:512], in1=ssK2[xb][rr, :], op=ALU.mult),
                  reads=["B%d" % bk2, "K%dss" % xb], writes=["rt2"])
            P.add("dve", lambda e, sb=sb: e.tensor_tensor(out=krst[sb][rr, :], in0=rt1[rr, :], in1=rt2[rr, :], op=ALU.add),
                  reads=["rt1", "rt2"], writes=["krst%d" % sb])
            self.dma("sp", KRd[:, tok], krst[sb][rr, :], reads=["krst%d" % sb], writes=["KRd"])
            self.evac("act", lfst[sb][:], B[bk1][0:8, 0:512], reads=["B%d" % bk1], writes=["lfst%d" % sb])
            self.dma("sp", LFraw[c * 8:(c + 1) * 8, :], lfst[sb][:], reads=["lfst%d" % sb], writes=["LFraw%d" % c])
            lz.emit(3)
            for (wt, stg, nm, dd, wkey) in ((wfk, stgK, "stgK", KFd, "wfk"), (wfv, stgV, "stgV", VFd, "wfv")):
                for hp in range(4):
                    bk = self.bank_rot()
                    self.mm_group(B[bk][:, 0:512], [(wt[:, k, hp * 128:(hp + 1) * 128], hT[hb][:, k, :]) for k in range(8)],
                                  reads=hkeys + [wkey], writes=["B%d" % bk])
                    self.evac(self.evac_engine(), stg[sb][:, hp, :], B[bk][:, 0:512], reads=["B%d" % bk], writes=["%s%d_%d" % (nm, sb, hp)])
                    lz.emit(3)
                sk = ["%s%d_%d" % (nm, sb, hp) for hp in range(4)]
                for two in range(2):
                    self.dma("sp", dd[two::2, 0:64, tok].rearrange("h r t -> r h t"), stg[sb][two * 64:(two + 1) * 64, :, :],
                             reads=sk, writes=[nm + "d"])
            lz.flush()
            if 1 <= c <= 8:
                hp_ = c - 1
                col0 = 2 * D + hp_ * 512
                self.dma("pool", wah[:], wview(ac["w_ada"][:, col0:col0 + 512]), writes=["wah"])
                bka = self.bank_rot()

                def fn(e, bka=bka):
                    inst = None
                    for j in range(4):
                        for k in range(8):
                            inst = e.matmul(B[bka][:, j:j + 1], lhsT=wah[:, k, j * 128:(j + 1) * 128], rhs=ac["cact"][:, k:k + 1], start=(k == 0), stop=(k == 7))
                    return inst
                P.add("pe", fn, reads=["wah"], writes=["B%d" % bka])
                a0_ = 16 + 4 * hp_
                P.add("dve", lambda e, bka=bka, a0_=a0_: e.tensor_tensor(out=ac["ada"][:, a0_:a0_ + 4], in0=B[bka][:, 0:4], in1=ac["bada"][:, a0_:a0_ + 4], op=ALU.add),
                      reads=["B%d" % bka], writes=["ada_h%d" % hp_])
            if c + 1 < 16:
                trans(c + 1, srcs_next)
            def dst_fn2(f, ncol):
                return ckvT[:, f, 0:ncol], ["ckvT%d" % f]
            self.nt_trans(lsrcs, 2, dst_fn2, gkv, None)
            for h in range(8):
                bk = self.bank_rot()
                self.mm_group(B[bk][:, 0:512], [(wukv[:, k2, h * 128:(h + 1) * 128], ckvT[:, k2, :]) for k2 in range(2)],
                              reads=["ckvT0", "ckvT1", "wukv"], writes=["B%d" % bk])
                self.evac(self.evac_engine(), stgM[sb][:, h, :], B[bk][:, 0:512], reads=["B%d" % bk], writes=["stgM%d_%d" % (sb, h)])
            mk = ["stgM%d_%d" % (sb, h) for h in range(8)]
            self.dma("sp", KMd[:, :, tok].rearrange("h r t -> r h t"), stgM[sb][0:64, :, :], reads=mk, writes=["KMd"])
            self.dma("sp", VMd[:, :, tok].rearrange("h r t -> r h t"), stgM[sb][64:128, :, :], reads=mk, writes=["VMd"])
        tmp8b = A.alloc("tmp8b", [128, 8], F32)
        P.add("dve", lambda e: e.tensor_scalar_add(out=tmp8b[:], in0=ac["ada"][:, 32:40], scalar1=1.0), reads=["ada_h%d" % i for i in range(8)], writes=["tmp8b"])
        P.add("dve", lambda e: e.tensor_tensor(out=ac["gs2"][:], in0=tmp8b[:], in1=ac["gffn"][:], op=ALU.mult), reads=["tmp8b"], writes=["gs2"])

        bfg = A.alloc("bfg", [128, 1], F32)
        cmat = A.alloc("cmat", [128, 128], F32)
        nlf = A.alloc("nlf", [128, 512], F32)
        Fx = A.alloc("Fx", [128, 514], F32)
        carry = A.alloc("carry", [128, 1], F32)
        rem = A.alloc("rem", [128, 512], F32)
        parts = [A.alloc("part%d" % j, [128, 512], BF16) for j in range(3)]
        fqn = A.alloc("fqn", [128, SBW], F32)
        qparts = [A.alloc("qpart%d" % j, [128, SBW], BF16) for j in range(3)]
        self.dma("sp", bfg[:], b_fg_c, writes=["bfg"])
        self.dma("sp", cmat[:], cmat_d, writes=["cmat"])
        P.add("dve", lambda e: e.tensor_scalar(out=bfg[:], in0=bfg[:], scalar1=-1.0, scalar2=None, op0=ALU.mult), reads=["bfg"], writes=["bfg"])
        lk = ["LFraw%d" % c for c in range(16)]
        P.add("act", lambda e: e.activation(out=nlf[:], in_=LFraw[:], func=AF.Exp, scale=-1.0, bias=bfg[:, 0:1]), reads=lk + ["bfg"], writes=["nlf"])
        P.add("act", lambda e: e.activation(out=nlf[:], in_=nlf[:], func=AF.Ln, bias=1.0), reads=["nlf"], writes=["nlf"])
        ones_f = self.consts["ones_f"]
        P.add("dve", lambda e: e.tensor_tensor_scan(out=Fx[:, 2:514], data0=ones_f[:, 0:512], data1=nlf[:], initial=0.0, op0=ALU.mult, op1=ALU.add),
              reads=["nlf", "ones_f"], writes=["Fx"])
        P.add("pe", lambda e: e.matmul(B[4][:, 0:1], lhsT=cmat[:], rhs=Fx[:, 513:514], start=True, stop=True), reads=["cmat", "Fx"], writes=["B4"])
        P.add("dve", lambda e: e.tensor_copy(out=carry[:], in_=B[4][:, 0:1]), reads=["B4"], writes=["carry"])
        P.add("dve", lambda e: e.tensor_scalar(out=Fx[:, 2:514], in0=Fx[:, 2:514], scalar1=carry[:, 0:1], scalar2=None, op0=ALU.add),
              reads=["Fx", "carry"], writes=["Fx"])
        P.add("pool", lambda e: e.memset(Fx[0:8, 0:2], 0.0), writes=["Fxh0"])
        self.dma("sp", Fx[8:128, 0:2], Fx[0:120, 512:514], reads=["Fx"], writes=["Fxh"])

        def split3(src, tgt, rm, k, n, skeys):
            P.add("dve", lambda e: e.tensor_copy(out=tgt[0][:, 0:n], in_=src), reads=skeys, writes=[k + "p0"])
            P.add("dve", lambda e: e.tensor_tensor(out=rm[:, 0:n], in0=src, in1=tgt[0][:, 0:n], op=ALU.subtract), reads=skeys + [k + "p0"], writes=["rem"])
            P.add("dve", lambda e: e.tensor_copy(out=tgt[1][:, 0:n], in_=rm[:, 0:n]), reads=["rem"], writes=[k + "p1"])
            P.add("dve", lambda e: e.tensor_tensor(out=rm[:, 0:n], in0=rm[:, 0:n], in1=tgt[1][:, 0:n], op=ALU.subtract), reads=["rem", k + "p1"], writes=["rem"])
            P.add("dve", lambda e: e.tensor_copy(out=tgt[2][:, 0:n], in_=rm[:, 0:n]), reads=["rem"], writes=[k + "p2"])
        split3(Fx[:, 2:514], parts, rem, "K", 512, ["Fx"])
        for j in range(3):
            if os.environ.get("F_BATCH", "1") == "1":
                self.dma("sp", KFd[:, 64 + j, :].rearrange("h (c t) -> c h t", c=16), parts[j][:],
                         reads=["Kp%d" % j], writes=["KFd_f"])
            else:
                for c in range(16):
                    self.dma("sp" if (c % 2 == 0) else "pool", KFd[:, 64 + j, c * 512:(c + 1) * 512], parts[j][c * 8:(c + 1) * 8, :],
                             reads=["Kp%d" % j], writes=["KFd_f"])
        P.add("dve", lambda e: e.tensor_scalar(out=fqn[:], in0=Fx[:, 0:SBW], scalar1=flg[:, 0:1], scalar2=None, op0=ALU.mult),
              reads=["Fx", "Fxh", "Fxh0", "flg"], writes=["fqn"])
        for d in range(1, 4):
            P.add("dve", lambda e, d=d: e.scalar_tensor_tensor(out=fqn[:], in0=Fx[:, 128 * d:128 * d + SBW], scalar=flg[:, d:d + 1], in1=fqn[:], op0=ALU.mult, op1=ALU.add),
                  reads=["Fx", "Fxh", "fqn", "flg"], writes=["fqn"])
        P.add("dve", lambda e: e.tensor_scalar(out=fqn[:], in0=fqn[:], scalar1=-1.0, scalar2=None, op0=ALU.mult), reads=["fqn"], writes=["fqn"])
        split3(fqn[:], qparts, rem, "Q", SBW, ["fqn"])
        for j in range(3):
            self.dma("sp", FQd[j], qparts[j][:], reads=["Qp%d" % j], writes=["FQd"])

    def phase_q(self, x_own, pos_own, w_cq, w_fq, w_uq, w_uqs, gs1, sh1, gq, invf, sgn, Qreg, HTd, QFd):
        nc, P, A, B = self.nc, self.P, self.A, self.B
        self.invf_t, self.sgn_t = invf, sgn
        wcq, wfq, wuq, wuqs = self.qw["wcq"], self.qw["wfq"], self.qw["wuq"], self.qw["wuqs"]
        hTo = A.alloc("hTo", [128, 8, TOWN], BF16)
        cqT = A.alloc("cqT", [128, 3, TOWN], BF16)
        xt = [A.alloc("xq%d" % i, [128, 4, D], F32) for i in range(2)]
        junk = A.alloc("junkq", [128, D], BF16)
        scr = dict(ssq=A.alloc("ssqq", [128, 4], F32), rv=A.alloc("rvq", [128, 4], F32), rstd=A.alloc("rstdq", [128, 4], F32), junk=junk)
        junkq2 = A.alloc("junkq2", [128, 384], BF16)
        scr2 = dict(ssq=A.alloc("ssqq2", [128, 4], F32), rv=A.alloc("rvq2", [128, 4], F32), rstd=A.alloc("rstdq2", [128, 4], F32), junk=junkq2)
        latq = A.alloc("latq", [128, 4, 384], F32)
        ccQ = A.alloc("ccQ", [96, TOWN], F32)
        ssQ = A.alloc("ssQ", [96, TOWN], F32)
        posi = A.alloc("posiq", [96, TOWN], I32)
        tmp_i = A.alloc("tmp_iq", [96, 520], I32)
        tmp_f = A.alloc("tmp_fq", [96, 520], F32)
        ang = A.alloc("angq", [96, 520], F32)
        rt1 = A.alloc("rt1q", [96, QT], F32)
        rt2 = A.alloc("rt2q", [96, QT], F32)
        qfst = [A.alloc("qfst%d" % i, [64, TOWN], BF16) for i in range(2)]
        rr = slice(64, 96)
        for h in range(8):
            P.add("pool", lambda e, h=h: e.memset(Qreg[h][64:128, :], 0.0), writes=["Qr%d_%d" % (h, n) for n in range(NQT)])
        self.dma("sp", posi[rr, :], pos_own[0:1, :].partition_broadcast(32), writes=["Qposi"])
        for q4 in range(4):
            cs = slice(q4 * 520, (q4 + 1) * 520)
            self.rope_tables("dve", posi[rr, cs], 520, ccQ[rr, cs], ssQ[rr, cs], tmp_i[rr, :], tmp_f[rr, :], ang[rr, :], "Q")
        chunks = [(c * 512, 4, 128) for c in range(4)] + [(2048, 1, 32)]
        for ci, (t0, ntile, rows) in enumerate(chunks):
            xb = ci % 2
            if rows == 128:
                self.dma("sp", xt[xb][:], x_own[t0:t0 + 512, :].rearrange("(j p) d -> p j d", p=128), writes=["xq%d" % xb])
            else:
                self.dma("sp", xt[xb][0:32, 0, :], x_own[t0:t0 + 32, :], writes=["xq%d" % xb])
            tiles = [(xt[xb][0:rows, j, :], rows, ["xq%d" % xb], False, None) for j in range(ntile)]

            def dst_fn(f, ncol, t0=t0):
                return hTo[:, f, t0:t0 + ncol], ["hTo%d_%d" % (f, t0)]
            self.norm_transpose(tiles, 8, dst_fn, gs1, sh1, "qx", scr)
        hk_all = ["hTo%d_%d" % (f, t0) for f in range(8) for (t0, _, _) in chunks]
        self.dma("sp", HTd, hTo[:], reads=hk_all, writes=["HTd"])
        def fq_head(h):
            sb = h % 2
            for n in range(NQT):
                cs = slice(n * QT, (n + 1) * QT)
                bk = self.bank_rot(lo=6, n=2)
                self.mm_group(B[bk][0:64, 0:QT], [(wfq[:, k, h * 64:(h + 1) * 64], hTo[:, k, cs]) for k in range(8)],
                              reads=hk_all + ["wfq"], writes=["B%d" % bk])
                self.evac(self.evac_engine(), qfst[sb][:, cs], B[bk][0:64, 0:QT], scale=0.125, reads=["B%d" % bk], writes=["qfst%d_%d" % (sb, n)])
            self.dma("sp", QFd[h], qfst[sb][:], reads=["qfst%d_%d" % (sb, n) for n in range(NQT)], writes=["QFd%d" % h])
        fq_plan = [[0, 1], [2, 3], [4, 5], [6], [7]]
        for ci, (t0, ntile, rows) in enumerate(chunks):
            hk = ["hTo%d_%d" % (f, t0) for f in range(8)]
            for j in range(ntile):
                bk = 2 + j
                self.mm_group(B[bk][0:rows, 0:384], [(hTo[:, k, t0 + j * 128:t0 + j * 128 + rows], wcq[:, k, :]) for k in range(8)],
                              reads=hk + ["wcq"], writes=["B%d" % bk])
            ltiles = [(B[2 + j][0:rows, 0:384], rows, ["B%d" % (2 + j)], True, latq[0:rows, j, :]) for j in range(ntile)]

            def dst_fn2(f, ncol, t0=t0):
                return cqT[:, f, t0:t0 + ncol], ["cqT%d_%d" % (f, t0)]
            lsrcs = self.nt_stats(ltiles, 3, "ql", scr2)
            for h in fq_plan[ci]:
                fq_head(h)
            self.nt_trans(lsrcs, 3, dst_fn2, gq, None)
        ck_all = ["cqT%d_%d" % (f, t0) for f in range(3) for (t0, _, _) in chunks]
        for h in range(8):
            for n in range(NQT):
                cs = slice(n * QT, (n + 1) * QT)
                b1 = self.bank_rot()
                self.mm_group(B[b1][0:96, 0:QT], [(wuq[:, k, h * 96:(h + 1) * 96], cqT[:, k, cs]) for k in range(3)],
                              reads=ck_all + ["wuq"], writes=["B%d" % b1])
                b2 = self.bank_rot()
                self.mm_group(B[b2][0:96, 0:QT], [(wuqs[:, k, h * 96:(h + 1) * 96], cqT[:, k, cs]) for k in range(3)],
                              reads=ck_all + ["wuqs"], writes=["B%d" % b2])
                self.evac(self.evac_engine(), Qreg[h][0:64, cs], B[b1][0:64, 0:QT], reads=["B%d" % b1], writes=["Q%d_%d" % (h, n)])
                P.add("dve", lambda e, b1=b1, cs=cs: e.tensor_tensor(out=rt1[rr, :], in0=B[b1][rr, 0:QT], in1=ccQ[rr, cs], op=ALU.mult),
                      reads=["B%d" % b1, "Qcc"], writes=["rt1q"])
                P.add("dve", lambda e, b2=b2, cs=cs: e.tensor_tensor(out=rt2[rr, :], in0=B[b2][rr, 0:QT], in1=ssQ[rr, cs], op=ALU.mult),
                      reads=["B%d" % b2, "Qss"], writes=["rt2q"])
                P.add("dve", lambda e, h=h, cs=cs: e.tensor_tensor(out=Qreg[h][rr, cs], in0=rt1[rr, :], in1=rt2[rr, :], op=ALU.add),
                      reads=["rt1q", "rt2q"], writes=["Qr%d_%d" % (h, n)])

    def phase_att(self, Qreg, oT, maskb, hm1b, KMd, KRd, VMd, KFd, VFd, FQd, QFd):
        nc, P, A, B = self.nc, self.P, self.A, self.B
        psS = self.psS
        ident_b, ones_f = self.consts["ident_b"], self.consts["ones_f"]
        Kb = [A.alloc("Kb%d" % i, [128, S], BF16) for i in range(2)]
        VTb = A.alloc("VTb", [64, S], BF16)
        Va = [A.alloc("Va%d" % i, [128, 64, 128], BF16) for i in range(2)]
        Pt = [A.alloc("Pt%d" % i, [128, 2, QT], BF16) for i in range(3)]
        osb = [A.alloc("osb%d" % i, [128, QT], F32) for i in range(2)]
        self.osb = osb
        rden = A.alloc("rden", [128, 2 * QT], F32)
        bc_sb = A.alloc("bc_sb", [64, QT], F32)
        Bb7 = self.psB_raw[1][:].bitcast(BF16)
        for i in range(2):
            P.add("pool", lambda e, i=i: e.memset(Va[i][:, :, (64 if i == 0 else 0):(128 if i == 0 else 64)], 1.0), writes=["Va%d_ones" % i])
            P.add("pool", lambda e, i=i: e.memset(Kb[i][64:128, :], 0.0), writes=["Kb%d" % i])
        unit = 0
        hcount = 0
        pre_hook = None
        for br in range(2):
            KR = 128
            scale = MLA_SCALE if br == 0 else 1.0
            def make_switch(h):
                def sw():
                    qk = ["Q%d_%d" % (h, n) for n in range(NQT)] + ["Qr%d_%d" % (h, n) for n in range(NQT)]
                    P.add("pool", lambda e: e.memset(Qreg[h][64:128, :], 0.0), reads=[], writes=qk)
                    P.add("pool", lambda e: e.memset(Qreg[h][64:70, :], 1.0), reads=[], writes=qk)
                    self.dma("sp", Qreg[h][0:64, :], QFd[h], writes=qk)
                    for j in range(3):
                        self.dma("sp", Qreg[h][64 + j:65 + j, :].rearrange("o (m i) -> o m i", i=SBW),
                                 FQd[j].rearrange("(m h) i -> h m i", h=8)[h:h + 1], writes=qk)
                return sw
            switch = [make_switch(h) for h in range(8)] if br == 0 else None
            if br == 1:
                for i in range(2):
                    P.add("pool", lambda e, i=i: e.memset(Kb[i][64:67, :], 1.0), writes=["Kb%d" % i])
            pairs = []
            loads = []
            vts = []
            for h in range(8):
                hb = hcount % 2
                hcount += 1
                qk = ["Q%d_%d" % (h, n) for n in range(NQT)] + ["Qr%d_%d" % (h, n) for n in range(NQT)]
                def load(h=h, hb=hb, br=br):
                    if br == 0:
                        self.dma("sp", Kb[hb][0:64, :], KMd[h], writes=["Kb%d" % hb])
                        self.dma("sp", Kb[hb][64:96, :], KRd, writes=["Kb%d" % hb])
                        self.dma("sp", VTb[:], VMd[h], writes=["VTb"])
                    else:
                        self.dma("sp", Kb[hb][0:64, :], KFd[h, 0:64, :], writes=["Kb%d" % hb])
                        self.dma("sp", Kb[hb][67:70, :], KFd[h, 64:67, :], writes=["Kb%d" % hb])
                        self.dma("sp", VTb[:], VFd[h], writes=["VTb"])

                def vtrans(hb=hb):
                    for rnd in range(4):
                        def fn(e, rnd=rnd):
                            inst = None
                            for i in range(16):
                                t = rnd * 16 + i
                                inst = e.transpose(out=Bb7[:, i * 64:(i + 1) * 64], in_=VTb[0:64, t * 128:(t + 1) * 128], identity=ident_b[0:64, 0:64])
                            return inst
                        P.add("pe", fn, reads=["VTb", "ident_b"], writes=["B7"])
                        P.add("dve", lambda e, rnd=rnd, hb=hb: e.tensor_copy(out=Va[hb][:, rnd * 16:(rnd + 1) * 16, (0 if hb == 0 else 64):(64 if hb == 0 else 128)],
                                                                              in_=Bb7.rearrange("p (t d) -> p t d", d=64)),
                              reads=["B7"], writes=["Va%d" % hb])
                loads.append(load)
                vts.append(vtrans)
                for Mq in range(NQT):
                    ob = 6
                    ou = unit % 2
                    unit += 1
                    npair = (8 * Mq + 8) // 2
                    for p in range(npair):
                        pairs.append(dict(h=h, hb=hb, Mq=Mq, p=p, ob=ob, ou=ou, last=(p == npair - 1), KR=KR, scale=scale, br=br, qk=qk,
                                          first=(Mq == 0 and p == 0)))
            self.att_stream(pairs, loads, vts, Kb, Va, Pt, Qreg, oT, maskb, hm1b, rden, bc_sb, switch, pre_hook)
            pre_hook = switch[7] if switch is not None else None

    def att_stream(self, pairs, loads, vts, Kb, Va, Pt, Qreg, oT, maskb, hm1b, rden, bc_sb, switch=None, pre_hook=None):
        P, B, psS = self.P, self.B, self.psS
        ident_b, ones_f = self.consts["ident_b"], self.consts["ones_f"]

        loads[0]()
        if pre_hook is not None:
            pre_hook()

        def qk_op(g):
            d_ = pairs[g]
            Mq, p, hb, h, KR, qk = d_["Mq"], d_["p"], d_["hb"], d_["h"], d_["KR"], d_["qk"]
            if d_["first"]:
                vts[h]()
                if h + 1 < len(loads):
                    loads[h + 1]()
                if switch is not None and h >= 1:
                    switch[h - 1]()
            q0 = Mq * QT
            sbuf = g % 3
            wide = (2 * p) < 8 * Mq + 4

            def fn(e):
                inst = None
                for j in range(2):
                    kb = 2 * p + j
                    lhsT = Kb[hb][0:KR, kb * 128:(kb + 1) * 128]
                    extra = []
                    if wide:
                        out = psS[sbuf][:, j, 0:QT]
                        rhs = Qreg[h][0:KR, q0:q0 + QT]
                        if kb == 8 * Mq - 1:
                            extra.append((psS[sbuf][:, j, 0:2], hm1b[:, 0:2]))
                        d = kb - 8 * Mq
                        if 0 <= d <= 3:
                            extra.append((psS[sbuf][:, j, 0:SBW], maskb[:, d, :]))
                            if d == 3:
                                extra.append((psS[sbuf][:, j, SBW:SBW + 2], hm1b[:, 0:2]))
                    else:
                        out = psS[sbuf][:, j, 0:SBW]
                        rhs = Qreg[h][0:KR, q0 + SBW:q0 + QT]
                        d = kb - 8 * Mq - 4
                        extra.append((psS[sbuf][:, j, 0:SBW], maskb[:, d, :]))
                    inst = e.matmul(out, lhsT=lhsT, rhs=rhs, start=True, stop=(len(extra) == 0))
                    for xi, (xo, xr) in enumerate(extra):
                        inst = e.matmul(xo, lhsT=ident_b[:], rhs=xr, start=False, stop=(xi == len(extra) - 1))
                return inst
            P.add("pe", fn, reads=["Kb%d" % hb, "maskb", "hm1b", "ident_b"] + qk, writes=["S%d" % sbuf])

        def exp_op(g):
            d_ = pairs[g]
            sbuf = g % 3
            W = QT if (2 * d_["p"]) < 8 * d_["Mq"] + 4 else SBW
            scale = d_["scale"]
            P.add("act", lambda e: e.activation(out=Pt[sbuf][:, :, 0:W], in_=psS[sbuf][:, :, 0:W], func=AF.Exp, scale=scale),
                  reads=["S%d" % sbuf], writes=["P%d" % sbuf])

        def pv_op(g):
            d_ = pairs[g]
            Mq, p, hb, ob = d_["Mq"], d_["p"], d_["hb"], d_["ob"]
            nkb = 8 * Mq + 8
            sbuf = g % 3
            wide = (2 * p) < 8 * Mq + 4

            def fn(e):
                inst = None
                for j in range(2):
                    kb = 2 * p + j
                    if wide:
                        out = B[ob][:, 0:QT]
                        rhs = Pt[sbuf][:, j, 0:QT]
                    else:
                        out = B[ob][:, SBW:QT]
                        rhs = Pt[sbuf][:, j, 0:SBW]
                    inst = e.matmul(out, lhsT=Va[hb][:, kb, :], rhs=rhs, start=(kb == 0), stop=(kb == nkb - 1))
                return inst
            P.add("pe", fn, reads=["P%d" % sbuf, "Va%d" % hb, "Va%d_ones" % hb], writes=["B%d" % ob])

        pending = []

        def end_unit(g):
            d_ = pairs[g]
            br, h, Mq, ou, ob = d_["br"], d_["h"], d_["Mq"], d_["ou"], d_["ob"]
            q0 = Mq * QT
            osb = self.osb
            hb = d_["hb"]
            dr = 64 if hb == 0 else 0
            vr = slice(0, 64) if hb == 0 else slice(64, 128)
            rd = rden[dr:dr + 1, ou * QT:(ou + 1) * QT]
            P.add("dve", lambda e: e.tensor_copy(out=osb[ou][:, :], in_=B[ob][:, 0:QT]), reads=["B%d" % ob], writes=["osb%d" % ou])
            P.add("dve", lambda e: e.tensor_scalar(out=rd, in0=osb[ou][dr:dr + 1, :], scalar1=1e-30, scalar2=None, op0=ALU.max),
                  reads=["osb%d" % ou], writes=["rden%d" % ou])
            P.add("dve", lambda e: e.reciprocal(out=rd, in_=rd), reads=["rden%d" % ou], writes=["rden%d" % ou])

            def tail():
                P.add("pe", lambda e: e.matmul(B[7][:, 0:QT], lhsT=ones_f[dr:dr + 1, 0:128], rhs=rd, start=True, stop=True),
                      reads=["rden%d" % ou, "ones_f"], writes=["B7"])
                P.add("dve", lambda e: e.tensor_tensor(out=oT[br][h // 2][vr, q0:q0 + QT], in0=osb[ou][vr, :], in1=B[7][vr, 0:QT], op=ALU.mult),
                      reads=["osb%d" % ou, "B7"], writes=["oT%d_%d_%d" % (br, h, Mq)])
            pending.append((g + 8, tail))

        def flush_pending(g):
            while pending and pending[0][0] <= g:
                pending.pop(0)[1]()

        n = len(pairs)
        import os
        nfill = int(os.environ.get("ATT_FILL", "0"))
        fillw = int(os.environ.get("ATT_FILLW", "128"))

        def filler(g):
            sbuf = g % 3
            for i in range(nfill):
                P.add("pe", lambda e, i=i: e.matmul(B[7][:, 0:fillw], lhsT=ident_b[:], rhs=maskb[:].rearrange("p a b -> p (a b)")[:, 0:fillw], start=True, stop=True),
                      reads=[], writes=[])
        for g in range(min(3, n)):
            qk_op(g)
        for g in range(n):
            exp_op(g)
            if g + 3 < n:
                qk_op(g + 3)
            pv_op(g)
            flush_pending(g)
            if pairs[g]["last"]:
                end_unit(g)
            filler(g)
        flush_pending(n + 100)

    def phase_merge1(self, oT, yT, HTd, w_om, w_of, w_gm, w_gf):
        nc, P, A, B = self.nc, self.P, self.A, self.B
        hTo = A.alloc("hTo2", [128, 8, TOWN], BF16)
        self.dma("sp", hTo[:], HTd, writes=["hTo2"])
        wom = [A.alloc("wom%d" % i, [128, 4, 128], BF16) for i in range(2)]
        wof = [A.alloc("wof%d" % i, [128, 4, 128], BF16) for i in range(2)]
        wgm = [A.alloc("wgm%d" % i, [128, 8, 128], BF16) for i in range(2)]
        wgf = [A.alloc("wgf%d" % i, [128, 8, 128], BF16) for i in range(2)]
        sgm = [A.alloc("sgm%d" % i, [128, QT], F32) for i in range(2)]
        sgf = [A.alloc("sgf%d" % i, [128, QT], F32) for i in range(2)]
        t1 = [A.alloc("t1_%d" % i, [128, QT], F32) for i in range(2)]
        t2 = [A.alloc("t2_%d" % i, [128, QT], F32) for i in range(2)]
        it = 0
        for dc in range(8):
            wb = dc % 2
            dcs = slice(dc * 128, (dc + 1) * 128)
            self.dma("pool", wom[wb][:], w_om[:, dcs].rearrange("(h r) c -> r h c", r=128), writes=["wom%d" % wb])
            self.dma("pool", wof[wb][:], w_of[:, dcs].rearrange("(h r) c -> r h c", r=128), writes=["wof%d" % wb])
            self.dma("pool", wgm[wb][:], wview(w_gm[:, dcs]), writes=["wgm%d" % wb])
            self.dma("pool", wgf[wb][:], wview(w_gf[:, dcs]), writes=["wgf%d" % wb])
            for n in range(NQT):
                cs = slice(n * QT, (n + 1) * QT)
                st = it % 2
                it += 1
                b0 = 4 * st
                self.mm_group(B[b0][:, 0:QT], [(wom[wb][:, hp, :], oT[0][hp][:, cs]) for hp in range(4)], reads=["wom%d" % wb], writes=["B%d" % b0])
                self.mm_group(B[b0 + 1][:, 0:QT], [(wof[wb][:, hp, :], oT[1][hp][:, cs]) for hp in range(4)], reads=["wof%d" % wb], writes=["B%d" % (b0 + 1)])
                self.mm_group(B[b0 + 2][:, 0:QT], [(wgm[wb][:, k, :], hTo[:, k, cs]) for k in range(8)], reads=["wgm%d" % wb, "hTo2"], writes=["B%d" % (b0 + 2)])
                self.mm_group(B[b0 + 3][:, 0:QT], [(wgf[wb][:, k, :], hTo[:, k, cs]) for k in range(8)], reads=["wgf%d" % wb, "hTo2"], writes=["B%d" % (b0 + 3)])
                P.add("act", lambda e, st=st, b0=b0: e.activation(out=sgm[st][:], in_=B[b0 + 2][:, 0:QT], func=AF.Sigmoid), reads=["B%d" % (b0 + 2)], writes=["sgm%d" % st])
                P.add("act", lambda e, st=st, b0=b0: e.activation(out=sgf[st][:], in_=B[b0 + 3][:, 0:QT], func=AF.Sigmoid), reads=["B%d" % (b0 + 3)], writes=["sgf%d" % st])
                P.add("dve", lambda e, st=st, b0=b0: e.tensor_tensor(out=t1[st][:], in0=sgm[st][:], in1=B[b0][:, 0:QT], op=ALU.mult), reads=["sgm%d" % st, "B%d" % b0], writes=["t1_%d" % st])
                P.add("dve", lambda e, st=st, b0=b0: e.tensor_tensor(out=t2[st][:], in0=sgf[st][:], in1=B[b0 + 1][:, 0:QT], op=ALU.mult), reads=["sgf%d" % st, "B%d" % (b0 + 1)], writes=["t2_%d" % st])
                P.add("dve", lambda e, st=st, dc=dc, cs=cs: e.tensor_tensor(out=yT[:, dc, cs], in0=t1[st][:], in1=t2[st][:], op=ALU.add), reads=["t1_%d" % st, "t2_%d" % st], writes=["yT%d_%d" % (dc, n)])

    def phase_merge2(self, xT, yT, x_own, w_out, gm):
        nc, P, A, B = self.nc, self.P, self.A, self.B
        ident_f = self.consts["ident_f"]
        wout = A.alloc("wout", [128, 8, D], BF16)
        self.dma("pool", wout[:], wview(w_out), writes=["wout"])
        xt = [A.alloc("xr%d" % i, [128, 4, D], F32) for i in range(2)]
        chunks = [(c * 512, 4, 128) for c in range(4)] + [(2048, 1, 32)]
        for ci, (t0, ntile, rows) in enumerate(chunks):
            xb = ci % 2
            if rows == 128:
                self.dma("sp", xt[xb][:], x_own[t0:t0 + 512, :].rearrange("(j p) d -> p j d", p=128), writes=["xr%d" % xb])
            else:
                self.dma("sp", xt[xb][0:32, 0, :], x_own[t0:t0 + 32, :], writes=["xr%d" % xb])
            ncol = ntile * rows
            for f in range(8):
                bk = f % 4

                def fn(e, f=f, bk=bk, xb=xb, ntile=ntile, rows=rows):
                    inst = None
                    for j in range(ntile):
                        inst = e.transpose(out=B[bk][:, j * rows:(j + 1) * rows], in_=xt[xb][0:rows, j, f * 128:(f + 1) * 128], identity=ident_f[0:rows, 0:rows])
                    return inst
                P.add("pe", fn, reads=["xr%d" % xb, "ident_f"], writes=["B%d" % bk])
                self.evac(self.evac_engine(), xT[:, f, t0:t0 + ncol], B[bk][:, 0:ncol], reads=["B%d" % bk], writes=["xT%d_%d" % (f, ci)])
        xk = ["xT%d_%d" % (f, ci) for f in range(8) for ci in range(5)]
        for dc in range(8):
            for n in range(NQT):
                cs = slice(n * QT, (n + 1) * QT)
                bk = 4 + (dc * 8 + n) % 4
                self.mm_group(B[bk][:, 0:QT], [(wout[:, k, dc * 128:(dc + 1) * 128], yT[:, k, cs]) for k in range(8)], reads=["wout"], writes=["B%d" % bk])
                P.add("dve", lambda e, bk=bk, dc=dc, cs=cs: e.scalar_tensor_tensor(out=xT[:, dc, cs], in0=B[bk][:, 0:QT], scalar=gm[:, dc:dc + 1], in1=xT[:, dc, cs], op0=ALU.mult, op1=ALU.add),
                      reads=["B%d" % bk] + xk, writes=["x1T%d_%d" % (dc, n)])

    def phase_ffn(self, xT, w_up, w_down, conv_w_c, conv_b_c, g_fin_r, gs2, sh2, gf, hval, out_d):
        nc, P, A, B = self.nc, self.P, self.A, self.B
        psS = self.psS
        ident_f, ones_f = self.consts["ident_f"], self.consts["ones_f"]
        h2T = A.alloc("h2T", [128, 8, TOWN], BF16)
        cw = A.alloc("cw", [128, 44, 3], F32)
        cb = A.alloc("cb", [128, 44], F32)
        gfin = A.alloc("gfin", [128, D], F32)
        self.dma("sp", cw[:], conv_w_c, writes=["cw"])
        self.dma("sp", cb[:], conv_b_c, writes=["cb"])
        self.dma("sp", gfin[:], g_fin_r[0:1, :].partition_broadcast(128), writes=["gfin"])
        f_mark = A.mark()
        sq = A.alloc("sq", [128, 8, QT], BF16)
        ones_b = A.alloc("ones_b", [128, 8], BF16)
        P.add("pool", lambda e: e.memset(ones_b[:], 1.0), writes=["ones_b"])
        tmpf = [A.alloc("tmpf%d" % i, [128, QT], F32) for i in range(2)]
        rowv = A.alloc("rowv", [1, TOWN], F32)
        for n in range(NQT):
            cs = slice(n * QT, (n + 1) * QT)
            P.add("dve", lambda e, cs=cs: e.tensor_tensor(out=sq[:], in0=xT[:, :, cs], in1=xT[:, :, cs], op=ALU.mult), reads=[], writes=["sq"])
            bk = 4 + n % 2
            self.mm_group(B[bk][0:1, 0:QT], [(ones_b[:, 0:1], sq[:, k, :]) for k in range(8)], reads=["sq", "ones_b"], writes=["B%d" % bk])
            P.add("dve", lambda e, bk=bk, cs=cs: e.tensor_scalar(out=rowv[0:1, cs], in0=B[bk][0:1, 0:QT], scalar1=1.0 / D, scalar2=EPS, op0=ALU.mult, op1=ALU.add),
                  reads=["B%d" % bk], writes=["rowv%d" % n])
            P.add("act", lambda e, cs=cs: e.activation(out=rowv[0:1, cs], in_=rowv[0:1, cs], func=AF.Sqrt), reads=["rowv%d" % n], writes=["rowv%d" % n])
            P.add("dve", lambda e, cs=cs: e.reciprocal(out=rowv[0:1, cs], in_=rowv[0:1, cs]), reads=["rowv%d" % n], writes=["rowv%d" % n])
        it = 0
        for n in range(NQT):
            cs = slice(n * QT, (n + 1) * QT)
            bk = 6 + n % 2
            P.add("pe", lambda e, bk=bk, cs=cs: e.matmul(B[bk][:, 0:QT], lhsT=ones_f[0:1, 0:128], rhs=rowv[0:1, cs], start=True, stop=True),
                  reads=["rowv%d" % n, "ones_f"], writes=["B%d" % bk])
            for k in range(8):
                tb = it % 2
                it += 1
                P.add("dve", lambda e, bk=bk, cs=cs, k=k, tb=tb: e.scalar_tensor_tensor(out=tmpf[tb][:], in0=xT[:, k, cs], scalar=gs2[:, k:k + 1], in1=B[bk][:, 0:QT], op0=ALU.mult, op1=ALU.mult),
                      reads=["B%d" % bk], writes=["tmpf%d" % tb])
                P.add("act", lambda e, cs=cs, k=k, tb=tb: e.activation(out=h2T[:, k, cs], in_=tmpf[tb][:], func=AF.Identity, bias=sh2[:, k:k + 1]),
                      reads=["tmpf%d" % tb], writes=["h2T%d_%d" % (k, n)])
        P.barrier()
        A.reset(f_mark)
        if self.stop_after == "F1":
            self.h2T_dbg = h2T
            return
        NSC = 2
        NPS = NQT // NSC
        aT = A.alloc("aT", [128, NFF, NPS * 256], BF16)
        wg = [A.alloc("wg%d" % i, [128, 8, 128], BF16) for i in range(2)]
        wv = [A.alloc("wv%d" % i, [128, 8, 128], BF16) for i in range(2)]
        wdn = [A.alloc("wdn%d" % i, [128, NFF, 128], BF16) for i in range(2)]
        cg = [A.alloc("cg%d" % i, [128, 2, 128], F32) for i in range(2)]
        cv = [A.alloc("cv%d" % i, [128, 2, 128], F32) for i in range(2)]
        sg = [A.alloc("sg%d" % i, [128, 2, 128], F32) for i in range(2)]
        ot = [A.alloc("ot%d" % i, [128, D], F32) for i in range(2)]
        ssqo = A.alloc("ssqo", [128, 2], F32)
        rvo = A.alloc("rvo", [128, 2], F32)
        rso = A.alloc("rso", [128, 2], F32)
        junko = A.alloc("junko", [128, D], BF16)
        it = 0
        wi = 0
        di = 0
        oi = 0
        for sc in range(NSC):
            for ffc in range(NFF):
                wb = wi % 2
                wi += 1
                self.dma("pool", wg[wb][:], wview(w_up[:, ffc * 128:(ffc + 1) * 128]), writes=["wg%d" % wb])
                self.dma("pool", wv[wb][:], wview(w_up[:, DFF + ffc * 128:DFF + (ffc + 1) * 128]), writes=["wv%d" % wb])
                for nl in range(NPS):
                    n = sc * NPS + nl
                    cs = slice(n * QT, (n + 1) * QT)
                    tb = it % 2
                    it += 1
                    bg = 4 + 2 * tb
                    bv = bg + 1
                    hk = ["h2T%d_%d" % (k, n) for k in range(8)]
                    self.mm_group(B[bg][:, 0:QT], [(wg[wb][:, k, :], h2T[:, k, cs]) for k in range(8)], reads=["wg%d" % wb], writes=["B%d" % bg])
                    self.mm_group(B[bv][:, 0:QT], [(wv[wb][:, k, :], h2T[:, k, cs]) for k in range(8)], reads=["wv%d" % wb], writes=["B%d" % bv])
                    if n == 0:
                        for bb in (bg, bv):
                            P.add("dve", lambda e, bb=bb: e.tensor_scalar(out=B[bb][:, 0:2], in0=B[bb][:, 0:2], scalar1=hval[:, 0:1], scalar2=None, op0=ALU.mult),
                                  reads=["B%d" % bb], writes=["B%d" % bb])
                    for (bb, cidx, dst, nm) in ((bg, ffc, cg, "cg"), (bv, NFF + ffc, cv, "cv")):
                        u3 = B[bb][:, 0:QT].rearrange("p (s i) -> p s i", i=SBW)
                        P.add("act", lambda e, u3=u3, cidx=cidx, dst=dst, tb=tb: e.activation(out=dst[tb][:], in_=u3[:, :, 2:130], func=AF.Identity, scale=cw[:, cidx, 2:3], bias=cb[:, cidx:cidx + 1]),
                              reads=["B%d" % bb, "cw", "cb"], writes=["%s%d" % (nm, tb)])
                        P.add("dve", lambda e, u3=u3, cidx=cidx, dst=dst, tb=tb: e.scalar_tensor_tensor(out=dst[tb][:], in0=u3[:, :, 1:129], scalar=cw[:, cidx, 1:2], in1=dst[tb][:], op0=ALU.mult, op1=ALU.add),
                              reads=["B%d" % bb, "%s%d" % (nm, tb)], writes=["%s%d" % (nm, tb)])
                        P.add("dve", lambda e, u3=u3, cidx=cidx, dst=dst, tb=tb: e.scalar_tensor_tensor(out=dst[tb][:], in0=u3[:, :, 0:128], scalar=cw[:, cidx, 0:1], in1=dst[tb][:], op0=ALU.mult, op1=ALU.add),
                              reads=["B%d" % bb, "%s%d" % (nm, tb)], writes=["%s%d" % (nm, tb)])
                    P.add("act", lambda e, tb=tb: e.activation(out=sg[tb][:], in_=cg[tb][:], func=AF.Silu), reads=["cg%d" % tb], writes=["sg%d" % tb])
                    P.add("dve", lambda e, tb=tb, ffc=ffc, nl=nl: e.tensor_tensor(out=aT[:, ffc, nl * 256:(nl + 1) * 256].rearrange("p (s i) -> p s i", i=128), in0=sg[tb][:], in1=cv[tb][:], op=ALU.mult),
                          reads=["sg%d" % tb, "cv%d" % tb], writes=["aT%d_%d" % (ffc, nl)])
            for dc in range(8):
                db = di % 2
                di += 1
                self.dma("pool", wdn[db][:], wview(w_down[:, dc * 128:(dc + 1) * 128]), writes=["wdn%d" % db])
                for nl in range(NPS):
                    n = sc * NPS + nl
                    cs = slice(n * QT, (n + 1) * QT)
                    bk = 4 + (dc * NPS + nl) % 4
                    self.mm_group(B[bk][:, 0:256], [(wdn[db][:, ffc, :], aT[:, ffc, nl * 256:(nl + 1) * 256]) for ffc in range(NFF)],
                                  reads=["wdn%d" % db] + ["aT%d_%d" % (ffc, nl) for ffc in range(NFF)], writes=["B%d" % bk])
                    xv = xT[:, dc, cs].rearrange("p (s i) -> p s i", i=SBW)[:, :, 2:130]
                    P.add("dve", lambda e, bk=bk, dc=dc, xv=xv: e.scalar_tensor_tensor(out=xv, in0=B[bk][:, 0:256].rearrange("p (s i) -> p s i", i=128), scalar=gf[:, dc:dc + 1], in1=xv, op0=ALU.mult, op1=ALU.add),
                          reads=["B%d" % bk], writes=["x2T%d_%d" % (dc, n)])
            for nl in range(NPS):
                n = sc * NPS + nl
                for s_ in range(2):
                    m = 2 * n + s_
                    ob = oi % 2
                    oi += 1
                    c0 = m * SBW + 2
                    flat = psS[ob][:].rearrange("p a b -> p (a b)")

                    def fn(e, ob=ob, c0=c0):
                        inst = None
                        for dc in range(8):
                            inst = e.transpose(out=psS[ob][:, dc // 4, (dc % 4) * 128:(dc % 4 + 1) * 128], in_=xT[:, dc, c0:c0 + 128], identity=ident_f[:])
                        return inst
                    P.add("pe", fn, reads=["x2T%d_%d" % (dc, n) for dc in range(8)] + ["ident_f"], writes=["S%d" % ob])
                    P.add("act", lambda e, flat=flat, ob=ob: e.activation(out=junko[:], in_=flat, func=AF.Square, accum_out=ssqo[:, ob:ob + 1]),
                          reads=["S%d" % ob], writes=["ssqo%d" % ob, "junko"])
                    P.add("dve", lambda e, ob=ob: e.tensor_scalar(out=rvo[:, ob:ob + 1], in0=ssqo[:, ob:ob + 1], scalar1=1.0 / D, scalar2=EPS, op0=ALU.mult, op1=ALU.add),
                          reads=["ssqo%d" % ob], writes=["rvo%d" % ob])
                    P.add("pool", lambda e, ob=ob: e.tensor_tensor(out=rso[:, ob:ob + 1], in0=rvo[:, ob:ob + 1], in1=self.mhalf[:, 0:1], op=ALU.pow),
                          reads=["rvo%d" % ob, "mhalf"], writes=["rso%d" % ob])
                    P.add("dve", lambda e, flat=flat, ob=ob: e.scalar_tensor_tensor(out=ot[ob][:], in0=flat, scalar=rso[:, ob:ob + 1], in1=gfin[:], op0=ALU.mult, op1=ALU.mult),
                          reads=["S%d" % ob, "rso%d" % ob, "gfin"], writes=["ot%d" % ob])
                    self.dma("sp", out_d[m * 128:(m + 1) * 128, :], ot[ob][:], reads=["ot%d" % ob], writes=["out%d" % m])

    def finish_dbg(self, dumps):
        nc, P = self.nc, self.P
        self.dbg_mark = self.A.mark()
        for name, (t, shape, dt) in dumps.items():
            d = nc.dram_tensor("dbg_" + name, list(shape), F32, kind="ExternalOutput").ap()
            if dt == F32:
                self.dma("sp", d, t[:], reads=[], writes=["dbg_" + name])
            else:
                tf = self.A.alloc("dbgf_" + name, list(shape), F32)
                P.add("dve", lambda e, tf=tf, t=t: e.tensor_copy(out=tf[:], in_=t[:]), writes=["dbgf_" + name])
                self.dma("sp", d, tf[:], reads=["dbgf_" + name], writes=["dbg_" + name])
        for name in self.dump_scr:
            ap = self.scr[name]
            shp = list(ap.shape)
            d = nc.dram_tensor("dbg_" + name, shp, F32, kind="ExternalOutput").ap()
            if len(shp) == 2:
                srcs = [(ap, d)]
            else:
                srcs = [(ap[i], d[i]) for i in range(shp[0])]
            for i, (sa, da) in enumerate(srcs):
                rows, cols = sa.shape
                tf = self.A.alloc("dbgs", [rows, cols], F32)
                self.dma("pool", tf[:], sa, writes=["dbgs%s%d" % (name, i)])
                self.dma("sp", da, tf[:], reads=["dbgs%s%d" % (name, i)], writes=["dbgo%s%d" % (name, i)])
                if (i % 4) == 3:
                    P.barrier()
                    self.A.reset(self.dbg_mark)
        self.dma("sp", self.out_d[0:128, 0:128], self.consts["ident_f"][:], writes=["outd"])
        jt = self.A.alloc("touch", [1, 64], F32)
        jti = self.A.alloc("touchi", [1, 64], I32)
        for i, (name, ap) in enumerate(self.din.items()):
            idx = tuple([slice(0, 1)] * len(ap.shape))
            src = ap[idx]
            while len(src.shape) > 2:
                src = src[0]
            tgt = jti if ap.dtype == I32 else jt
            self.dma("sp", tgt[0:1, i:i + 1], src, writes=["touch%d" % i])
        P.barrier()
        P.emit()
        return nc


def prep_inputs(inp):
    x = np.asarray(inp["x"], np.float32)
    cvec = np.asarray(inp["c"], np.float32)
    pos = np.asarray(inp["positions"], np.int32)
    w_in = np.asarray(inp["w_in"], np.float32)[0]
    o = [0, 384, 640, 672, 1184, 1696, 2208, 2216, 3240, 4264]
    w_cq, w_ckv_, w_krope, w_fq, w_fk, w_fv, w_fl, w_gm, w_gf = [np.ascontiguousarray(w_in[:, o[i]:o[i + 1]]) for i in range(9)]
    swap = np.concatenate([np.arange(16, 32), np.arange(0, 16)])
    w_kr = np.zeros((D, 96), np.float32)
    w_kr[:, 64:96] = w_krope
    w_kr[:, 0:8] = w_fl
    w_krs = np.zeros((D, 96), np.float32)
    w_krs[:, 64:96] = w_krope[:, swap]
    w_uq = np.asarray(inp["w_uq"], np.float32)[0]
    w_uqs = w_uq.copy().reshape(384, 8, 96)
    w_uqs[:, :, 64:96] = w_uqs[:, :, 64:96][:, :, swap]
    w_uqs = np.ascontiguousarray(w_uqs.reshape(384, 768))

    def col(v, n):
        return np.ascontiguousarray(np.asarray(v, np.float32).reshape(n, 128).T)
    invf = (10000.0 ** (-np.arange(0, 32, 2, dtype=np.float32) / np.float32(32))).astype(np.float32)
    invf_c = np.zeros((96, 1), np.float32)
    invf_c[64:80, 0] = invf
    invf_c[80:96, 0] = invf
    sgn_c = np.zeros((96, 1), np.float32)
    sgn_c[64:80] = -1.0
    sgn_c[80:96] = 1.0
    cmat = np.zeros((128, 128), np.float32)
    for cc in range(16):
        for h in range(8):
            for c2 in range(cc + 1, 16):
                cmat[cc * 8 + h, c2 * 8 + h] = 1.0
    conv_w = np.asarray(inp["conv_w"], np.float32)[0]
    conv_w_c = np.ascontiguousarray(conv_w.reshape(3, 44, 128).transpose(2, 1, 0))
    common = dict(
        w_ada=np.asarray(inp["w_ada"], np.float32)[0], b_ada_c=col(inp["b_ada"][0], 48),
        g_mix_c=col(inp["norm_mix_g"][0], 8), g_ffn_c=col(inp["norm_ffn_g"][0], 8),
        g_fin_r=np.asarray(inp["norm_final_g"], np.float32).reshape(1, D),
        g_q_c=col(inp["q_norm_g"][0], 3), g_kv_c=col(inp["kv_norm_g"][0], 2),
        b_fg_c=np.ascontiguousarray(np.tile(np.asarray(inp["b_forget"], np.float32)[0], 16).reshape(128, 1)),
        conv_w_c=conv_w_c, conv_b_c=col(inp["conv_b"][0], 44),
        w_ckv=w_ckv_, w_kr=w_kr, w_krs=w_krs, w_fk=w_fk, w_fv=w_fv, w_fl=w_fl, w_cq=w_cq, w_fq=w_fq, w_gm=w_gm, w_gf=w_gf,
        w_uq=w_uq, w_uqs=w_uqs, w_ukv=np.asarray(inp["w_ukv"], np.float32)[0],
        w_om=np.asarray(inp["w_o_mla"], np.float32)[0], w_of=np.asarray(inp["w_o_fox"], np.float32)[0],
        w_out=np.asarray(inp["w_out"], np.float32)[0], w_up=np.asarray(inp["w_up"], np.float32)[0],
        w_down=np.asarray(inp["w_down"], np.float32)[0],
        invf_c=invf_c, sgn_c=sgn_c, ident=np.eye(128, dtype=np.float32), cmat=cmat,
    )
    maps = []
    for b in range(2):
        for c in range(4):
            xo = np.zeros((NBLK, SBW, D), np.float32)
            po = np.zeros((NBLK, SBW), np.int32)
            for m in range(NBLK):
                r = 4 * m + c
                lo = 128 * r - 2
                if lo < 0:
                    xo[m, 2:] = x[b, 0:128]
                    po[m, 2:] = pos[b, 0:128]
                else:
                    xo[m] = x[b, lo:lo + SBW]
                    po[m] = pos[b, lo:lo + SBW]
            mk = np.zeros((128, 4, SBW), np.float32)
            sk = np.arange(128)[:, None]
            tq = np.arange(128)[None, :]
            for d in range(4):
                if d == c - 1:
                    mk[127, d, 0] = NEG
                elif d == c:
                    mk[:, d, 0:2] = NEG
                    mk[:, d, 2:] = np.where(sk > tq, NEG, 0.0)
                elif d > c:
                    mk[:, d, :] = NEG
            h1 = np.zeros((128, 2), np.float32)
            if c == 0:
                h1[127, 0] = NEG
            fl = np.zeros((128, 4), np.float32)
            fl[:, c] = 1.0
            hv = np.ones((128, 1), np.float32)
            m_ = dict(common)
            m_.update(x_kv=np.ascontiguousarray(x[b]), x_own=np.ascontiguousarray(xo.reshape(TOWN, D)),
                      pos_kv=np.ascontiguousarray(pos[b].reshape(1, S)), pos_own=np.ascontiguousarray(po.reshape(1, TOWN)),
                      c_col=col(cvec[b], 8), masks=mk, hm1=h1, flags=fl,
                      hvalid=(np.zeros((128, 1), np.float32) if c == 0 else hv))
            maps.append(m_)
    return maps


_NC_CACHE = {}


def kernel(**inputs):
    maps = prep_inputs(inputs)
    if "nc" not in _NC_CACHE:
        _NC_CACHE["nc"] = Builder().build()
    nc = _NC_CACHE["nc"]
    res = run_bass_kernel_spmd(nc, maps, core_ids=list(range(8)))
    out = np.zeros((2, S, D), np.float32)
    for b in range(2):
        for c in range(4):
            o = res.results[b * 4 + c]["out"].reshape(NBLK, 128, D)
            for m in range(NBLK):
                r = 4 * m + c
                out[b, 128 * r:128 * (r + 1)] = o[m]
    return out
```
